# Optimizing a Trainium2 kernel written in Bass

```python
import jax, jax.numpy as jnp
from jax import lax
import numpy as np

D_MODEL = 1024
BATCH = 2
SEQ = 8192
DEPTH = 4
DEC_BATCH = 32
DEC_SEQ = 32
PAST_LEN = 2048

CHUNK = 64
HEAD_DIM = 64
A_HEADS = 4
A_WIDTH = A_HEADS * HEAD_DIM
CONV_W = 4
LRU_C = 8.0
B_HEADS = 4
B_WIDTH = B_HEADS * HEAD_DIM
C_HEADS = 8
C_KV_HEADS = 2
C_GROUP = C_HEADS // C_KV_HEADS
C_WIDTH = C_HEADS * HEAD_DIM
C_KV_WIDTH = C_KV_HEADS * HEAD_DIM
WINDOW = 128
D_MIX = A_WIDTH + B_WIDTH + C_WIDTH
IN_SIZES = (A_WIDTH, A_WIDTH, B_WIDTH, B_WIDTH, B_WIDTH, B_WIDTH, C_WIDTH, C_KV_WIDTH, C_KV_WIDTH)
D_IN = sum(IN_SIZES)
D_FF = -(-8 * D_MODEL // (3 * 256)) * 256
ALPHA = (2.0 * DEPTH) ** 0.25
BETA = (8.0 * DEPTH) ** -0.25
LN_EPS = 1e-5
NEG = -1e30

kernel_name = 'hybrid_stream_rglru_retention_swa_step'


def layer_norm(x, g, b):
    xf = x.astype(jnp.float32)
    mu = jnp.mean(xf, axis=-1, keepdims=True)
    var = jnp.mean(jnp.square(xf - mu), axis=-1, keepdims=True)
    y = (xf - mu) * lax.rsqrt(var + LN_EPS) * g.astype(jnp.float32) + b.astype(jnp.float32)
    return y.astype(x.dtype)


def causal_conv(u, buf, w, b):
    T = u.shape[1]
    up = jnp.concatenate([buf.astype(u.dtype), u], axis=1)
    y = b + up[:, 0:T] * w[0]
    for j in range(1, CONV_W):
        y = y + up[:, j:j + T] * w[j]
    return y, up[:, T:]


def rg_lru(u, h0, w_a, b_a, w_x, b_x, lam):
    nb, T, _ = u.shape
    ub = u.reshape(nb, T, A_HEADS, HEAD_DIM)
    r = jax.nn.sigmoid(jnp.einsum('bthi,hij->bthj', ub, w_a).reshape(nb, T, A_WIDTH) + b_a)
    gi = jax.nn.sigmoid(jnp.einsum('bthi,hij->bthj', ub, w_x).reshape(nb, T, A_WIDTH) + b_x)
    log_a = -LRU_C * r.astype(jnp.float32) * jax.nn.softplus(-lam.astype(jnp.float32))
    a = jnp.exp(log_a)
    bt = jnp.sqrt(-jnp.expm1(2.0 * log_a)) * (gi * u).astype(jnp.float32)
    bt = bt.at[:, 0].add(a[:, 0] * h0.astype(jnp.float32))

    def combine(left, right):
        a_l, b_l = left
        a_r, b_r = right
        return a_l * a_r, a_r * b_l + b_r

    _, h = lax.associative_scan(combine, (a, bt), axis=1)
    return h.astype(u.dtype), h[:, -1].astype(u.dtype)


def retention_log_gamma():
    return jnp.log1p(-2.0 ** (-5.0 - jnp.arange(B_HEADS, dtype=jnp.float32)))


def retention(q, k, v, s0, chunk):
    nb, T, H, Dh = q.shape
    nc = T // chunk
    lg = retention_log_gamma()
    qc = q.reshape(nb, nc, chunk, H, Dh).astype(jnp.float32)
    kc = k.reshape(nb, nc, chunk, H, Dh).astype(jnp.float32) * Dh ** -0.5
    vc = v.reshape(nb, nc, chunk, H, Dh).astype(jnp.float32)
    idx = jnp.arange(chunk, dtype=jnp.float32)
    diff = idx[:, None] - idx[None, :]
    dmat = jnp.where(diff >= 0, jnp.exp(jnp.maximum(diff, 0.0)[None] * lg[:, None, None]), 0.0)
    inner = jnp.einsum('bnihd,bnjhd->bnhij', qc, kc) * dmat[None, None]
    o_inner = jnp.einsum('bnhij,bnjhe->bnihe', inner, vc)
    w_end = jnp.exp((chunk - 1.0 - idx)[None, :] * lg[:, None])
    u_chunk = jnp.einsum('bnjhd,hj,bnjhe->bnhde', kc, w_end, vc)
    g_chunk = jnp.exp(chunk * lg)[None, :, None, None]

    def step(s, u_n):
        return g_chunk * s + u_n, s

    s_last, s_prev = lax.scan(step, s0.astype(jnp.float32), jnp.moveaxis(u_chunk, 1, 0))
    s_prev = jnp.moveaxis(s_prev, 0, 1)
    w_start = jnp.exp((idx + 1.0)[None, :] * lg[:, None])
    o_cross = jnp.einsum('bnihd,hi,bnhde->bnihe', qc, w_start, s_prev)
    return (o_inner + o_cross).reshape(nb, T, H, Dh), s_last


def head_norm(o, g, b):
    nb, T, H, Dh = o.shape
    mu = jnp.mean(o, axis=-1, keepdims=True)
    var = jnp.mean(jnp.square(o - mu), axis=-1, keepdims=True)
    on = ((o - mu) * lax.rsqrt(var + LN_EPS)).reshape(nb, T, H * Dh)
    return on * g.astype(jnp.float32) + b.astype(jnp.float32)


def alibi_slopes():
    return 2.0 ** (-8.0 * jnp.arange(1, C_HEADS + 1, dtype=jnp.float32) / C_HEADS)


def sink_softmax(s, sink):
    m = jnp.maximum(jnp.max(s, axis=-1, keepdims=True), sink)
    e = jnp.exp(s - m)
    return e / (jnp.sum(e, axis=-1, keepdims=True) + jnp.exp(sink - m))


def swa_prompt(q, k, v, sinks):
    nb, T = q.shape[0], q.shape[1]
    nc = T // CHUNK
    nw = WINDOW // CHUNK
    kw_len = (nw + 1) * CHUNK
    qc = q.reshape(nb, nc, CHUNK, C_KV_HEADS, C_GROUP, HEAD_DIM)
    pad = ((0, 0), (nw, 0), (0, 0), (0, 0), (0, 0))
    kp = jnp.pad(k.reshape(nb, nc, CHUNK, C_KV_HEADS, HEAD_DIM), pad)
    vp = jnp.pad(v.reshape(nb, nc, CHUNK, C_KV_HEADS, HEAD_DIM), pad)
    kw = jnp.concatenate([kp[:, j:j + nc] for j in range(nw + 1)], axis=2)
    vw = jnp.concatenate([vp[:, j:j + nc] for j in range(nw + 1)], axis=2)
    s = jnp.einsum('bnqhgd,bnkhd->bnhgqk', qc, kw).astype(jnp.float32) * HEAD_DIM ** -0.5
    qpos = jnp.arange(nc)[:, None] * CHUNK + jnp.arange(CHUNK)[None, :]
    kpos = (jnp.arange(nc)[:, None] - nw) * CHUNK + jnp.arange(kw_len)[None, :]
    dist = jnp.abs(qpos[:, :, None] - kpos[:, None, :]).astype(jnp.float32)
    slopes = alibi_slopes().reshape(C_KV_HEADS, C_GROUP)
    s = s - slopes[None, None, :, :, None, None] * dist[None, :, None, None]
    s = jnp.where((kpos >= 0)[None, :, None, None, None, :], s, NEG)
    sk = sinks.astype(jnp.float32).reshape(C_KV_HEADS, C_GROUP)[None, None, :, :, None, None]
    pr = sink_softmax(s, sk)
    o = jnp.einsum('bnhgqk,bnkhd->bnqhgd', pr.astype(v.dtype), vw)
    return o.reshape(nb, T, C_WIDTH)


def swa_sample(q, k, v, k_past, v_past, sinks):
    nb, T = q.shape[0], q.shape[1]
    wc = k_past.shape[1]
    kk = jnp.concatenate([k_past.astype(k.dtype), k], axis=1)
    vv = jnp.concatenate([v_past.astype(v.dtype), v], axis=1)
    qg = q.reshape(nb, T, C_KV_HEADS, C_GROUP, HEAD_DIM)
    s = jnp.einsum('bqhgd,bkhd->bhgqk', qg, kk).astype(jnp.float32) * HEAD_DIM ** -0.5
    dist = jnp.abs((wc + jnp.arange(T))[:, None] - jnp.arange(wc + T)[None, :]).astype(jnp.float32)
    slopes = alibi_slopes().reshape(C_KV_HEADS, C_GROUP)
    s = s - slopes[None, :, :, None, None] * dist
    sk = sinks.astype(jnp.float32).reshape(C_KV_HEADS, C_GROUP)[None, :, :, None, None]
    pr = sink_softmax(s, sk)
    o = jnp.einsum('bhgqk,bkhd->bqhgd', pr.astype(v.dtype), vv)
    return o.reshape(nb, T, C_WIDTH)


def mixer(h, conv_buf, lru_h, ret_s, k_past, v_past, p, l):
    nb, T, _ = h.shape
    z = jnp.einsum('btd,de->bte', h, p['w_in'][l])
    split_at = np.cumsum(IN_SIZES)[:-1].tolist()
    ax, ag, bq, bk, bv, bg, cq, ck, cv = jnp.split(z, split_at, axis=-1)

    def heads(t, n):
        return t.reshape(nb, T, n, HEAD_DIM)

    u, conv_new = causal_conv(ax, conv_buf, p['conv_w'][l], p['conv_b'][l])
    ha, lru_new = rg_lru(u, lru_h, p['w_rg_a'][l], p['b_rg_a'][l], p['w_rg_x'][l], p['b_rg_x'][l], p['lru_lambda'][l])
    o_a = ha * jax.nn.gelu(ag)
    ro, ret_new = retention(heads(bq, B_HEADS), heads(bk, B_HEADS), heads(bv, B_HEADS), ret_s, min(CHUNK, T))
    o_b = head_norm(ro, p['ret_gn_g'][l], p['ret_gn_b'][l]).astype(h.dtype) * jax.nn.silu(bg)
    q, k, v = heads(cq, C_HEADS), heads(ck, C_KV_HEADS), heads(cv, C_KV_HEADS)
    if k_past is None:
        o_c = swa_prompt(q, k, v, p['sinks'][l])
        k_new, v_new = k[:, -WINDOW:], v[:, -WINDOW:]
    else:
        o_c = swa_sample(q, k, v, k_past, v_past, p['sinks'][l])
        k_new, v_new = k, v
    o = jnp.concatenate([o_a, o_b, o_c.astype(h.dtype)], axis=-1)
    y = jnp.einsum('bte,ed->btd', o, p['w_out'][l]).astype(h.dtype)
    return y, (conv_new, lru_new, ret_new.astype(h.dtype), k_new, v_new)


def swiglu(h, w_gate, w_up, w_down):
    a = jnp.einsum('btd,df->btf', h, w_gate)
    b = jnp.einsum('btd,df->btf', h, w_up)
    return jnp.einsum('btf,fd->btd', jax.nn.silu(a) * b, w_down).astype(h.dtype)


def trunk(x, st_conv, st_lru, st_ret, c_k, c_v, p):
    nb = x.shape[0]
    h = layer_norm(x, p['ln_in_g'], p['ln_in_b'])
    new = ([], [], [], [], [])
    for l in range(DEPTH):
        if c_k is None:
            conv_buf = jnp.zeros((nb, CONV_W - 1, A_WIDTH), h.dtype)
            lru_h = jnp.zeros((nb, A_WIDTH), h.dtype)
            ret_s = jnp.zeros((nb, B_HEADS, HEAD_DIM, HEAD_DIM), jnp.float32)
            k_past, v_past = None, None
        else:
            conv_buf, lru_h, ret_s, k_past, v_past = st_conv[l], st_lru[l], st_ret[l], c_k[l], c_v[l]
        m, states = mixer(h, conv_buf, lru_h, ret_s, k_past, v_past, p, l)
        h = layer_norm(ALPHA * h + m, p['ln1_g'][l], p['ln1_b'][l])
        f = swiglu(h, p['w_gate'][l], p['w_up'][l], p['w_down'][l])
        h = layer_norm(ALPHA * h + f, p['ln2_g'][l], p['ln2_b'][l])
        for lst, s in zip(new, states):
            lst.append(s)
    return h, tuple(jnp.stack(s) for s in new)


def setup_inputs(seed: int = 0) -> dict:
    key = jax.random.key(seed)
    ks = jax.random.split(key, 32)
    f32 = jnp.float32

    def nrm(k, shape, scale):
        return jax.random.normal(k, shape, f32) * scale

    wc = min(WINDOW, PAST_LEN)
    a_c = jax.random.uniform(ks[0], (DEPTH, A_WIDTH), f32, 0.9, 0.999)
    sig = a_c ** (1.0 / LRU_C)
    lru_lambda = jnp.log(sig) - jnp.log1p(-sig)
    return {
        'x_prompt': nrm(ks[1], (BATCH, SEQ, D_MODEL), 1.0),
        'x_sample': nrm(ks[2], (DEC_BATCH, DEC_SEQ, D_MODEL), 1.0),
        'state_conv': nrm(ks[3], (DEPTH, DEC_BATCH, CONV_W - 1, A_WIDTH), 1.0),
        'state_lru': nrm(ks[4], (DEPTH, DEC_BATCH, A_WIDTH), 0.5),
        'state_ret': nrm(ks[5], (DEPTH, DEC_BATCH, B_HEADS, HEAD_DIM, HEAD_DIM), 0.3),
        'cache_k': nrm(ks[6], (DEPTH, DEC_BATCH, wc, C_KV_HEADS, HEAD_DIM), 1.0),
        'cache_v': nrm(ks[7], (DEPTH, DEC_BATCH, wc, C_KV_HEADS, HEAD_DIM), 1.0),
        'ln_in_g': 1.0 + nrm(ks[8], (D_MODEL,), 0.02),
        'ln_in_b': nrm(ks[9], (D_MODEL,), 0.02),
        'w_in': nrm(ks[10], (DEPTH, D_MODEL, D_IN), D_MODEL ** -0.5),
        'conv_w': nrm(ks[11], (DEPTH, CONV_W, A_WIDTH), CONV_W ** -0.5),
        'conv_b': nrm(ks[12], (DEPTH, A_WIDTH), 0.02),
        'w_rg_a': nrm(ks[13], (DEPTH, A_HEADS, HEAD_DIM, HEAD_DIM), HEAD_DIM ** -0.5),
        'b_rg_a': nrm(ks[14], (DEPTH, A_WIDTH), 0.02),
        'w_rg_x': nrm(ks[15], (DEPTH, A_HEADS, HEAD_DIM, HEAD_DIM), HEAD_DIM ** -0.5),
        'b_rg_x': nrm(ks[16], (DEPTH, A_WIDTH), 0.02),
        'lru_lambda': lru_lambda,
        'ret_gn_g': 1.0 + nrm(ks[17], (DEPTH, B_WIDTH), 0.02),
        'ret_gn_b': nrm(ks[18], (DEPTH, B_WIDTH), 0.02),
        'sinks': nrm(ks[19], (DEPTH, C_HEADS), 0.5),
        'w_out': nrm(ks[20], (DEPTH, D_MIX, D_MODEL), D_MIX ** -0.5 * BETA),
        'ln1_g': 1.0 + nrm(ks[21], (DEPTH, D_MODEL), 0.02),
        'ln1_b': nrm(ks[22], (DEPTH, D_MODEL), 0.02),
        'w_gate': nrm(ks[23], (DEPTH, D_MODEL, D_FF), D_MODEL ** -0.5),
        'w_up': nrm(ks[24], (DEPTH, D_MODEL, D_FF), D_MODEL ** -0.5),
        'w_down': nrm(ks[25], (DEPTH, D_FF, D_MODEL), D_FF ** -0.5 * BETA),
        'ln2_g': 1.0 + nrm(ks[26], (DEPTH, D_MODEL), 0.02),
        'ln2_b': nrm(ks[27], (DEPTH, D_MODEL), 0.02),
    }


def reference(x_prompt, x_sample, state_conv, state_lru, state_ret, cache_k, cache_v,
              ln_in_g, ln_in_b, w_in, conv_w, conv_b, w_rg_a, b_rg_a, w_rg_x, b_rg_x, lru_lambda,
              ret_gn_g, ret_gn_b, sinks, w_out, ln1_g, ln1_b, w_gate, w_up, w_down, ln2_g, ln2_b):
    p = {'ln_in_g': ln_in_g, 'ln_in_b': ln_in_b, 'w_in': w_in, 'conv_w': conv_w, 'conv_b': conv_b,
         'w_rg_a': w_rg_a, 'b_rg_a': b_rg_a, 'w_rg_x': w_rg_x, 'b_rg_x': b_rg_x, 'lru_lambda': lru_lambda,
         'ret_gn_g': ret_gn_g, 'ret_gn_b': ret_gn_b, 'sinks': sinks, 'w_out': w_out,
         'ln1_g': ln1_g, 'ln1_b': ln1_b, 'w_gate': w_gate, 'w_up': w_up, 'w_down': w_down,
         'ln2_g': ln2_g, 'ln2_b': ln2_b}
    y_prompt, (p_conv, p_lru, p_ret, p_k, p_v) = trunk(x_prompt, None, None, None, None, None, p)
    y_sample, (s_conv, s_lru, s_ret, s_k, s_v) = trunk(x_sample, state_conv, state_lru, state_ret, cache_k, cache_v, p)
    return (y_prompt, y_sample, p_conv, p_lru, p_ret, p_k, p_v, s_conv, s_lru, s_ret, s_k, s_v)
```

```python
import numpy as np
from contextlib import ExitStack
import concourse.bass as bass
import concourse.mybir as mybir
from concourse.bass_utils import run_bass_kernel_spmd

F32 = mybir.dt.float32
BF16 = mybir.dt.bfloat16
AF = mybir.ActivationFunctionType
ALU = mybir.AluOpType

ENGS = ("pe", "dve", "act", "pool", "sp")
SEM_CH = 30000

D = 1024
DEPTH = 4
HD = 64
DFF = 2816
NFF = 22
ALPHA = (2.0 * DEPTH) ** 0.25
EPS = 1e-5
WINDOW = 128
NSQ = 4
ST = 32
BIG = 1.0e6


class Tok:
    __slots__ = ("name", "w", "r", "excl")

    def __init__(self, name="", excl=False):
        self.name = name
        self.w = None
        self.r = {}
        self.excl = excl


class Prog:
    def __init__(self, nc, n_dma_sems, n_slot_sems):
        self.nc = nc
        self.ins = {e: [] for e in ENGS}
        self.seen = {e: {} for e in ENGS}
        self.n_dma_sems = n_dma_sems
        self.n_slot = n_slot_sems
        self.dma_cnt = [0] * n_dma_sems
        self.dma_rr = n_slot_sems
        self.pool_rr = 0
        self.sp_rr = 0
        self.tag = ""
        self.tags = {e: [] for e in ENGS}
        self.rec = []
        self.reorder = True

    def _need(self, eng, ev, deps):
        if ev is None:
            return
        kind, a, b = ev
        if kind == "e" and a == eng and eng in ("pe", "sp"):
            return
        key = (kind, a)
        if self.seen[eng].get(key, -1) >= b:
            return
        self.seen[eng][key] = b
        deps.append(ev)

    def op(self, eng, fn, reads=(), writes=(), dma=False, dma_sem=None, cost=None):
        if cost is None:
            cost = 500.0 if (dma and eng == "pool") else (60.0 if dma else 300.0)
        self.rec.append((eng, fn, tuple(reads), tuple(writes), dma, dma_sem, self.tag, float(cost)))

    def schedule(self):
        import heapq
        rec = self.rec
        n = len(rec)
        preds = [None] * n
        lastw = {}
        readers = {}
        for i, (eng, fn, reads, writes, dma, dsem, tag, cost) in enumerate(rec):
            p = set()
            for t in reads:
                w = lastw.get(id(t))
                if w is not None:
                    p.add(w)
                if t.excl:
                    for j in readers.get(id(t), ()):
                        if rec[j][0] != eng:
                            p.add(j)
            for t in writes:
                w = lastw.get(id(t))
                if w is not None:
                    p.add(w)
                for j in readers.get(id(t), ()):
                    p.add(j)
            p.discard(i)
            preds[i] = p
            for t in writes:
                lastw[id(t)] = i
                readers[id(t)] = []
            for t in reads:
                if t not in writes:
                    readers.setdefault(id(t), []).append(i)
        succs = [[] for _ in range(n)]
        indeg = [0] * n
        for i in range(n):
            indeg[i] = len(preds[i])
            for j in preds[i]:
                succs[j].append(i)
        SEM_LAT = 50.0
        blev = [0.0] * n
        for i in range(n - 1, -1, -1):
            m = 0.0
            for k in succs[i]:
                if blev[k] > m:
                    m = blev[k]
            c = rec[i][7] + (2000.0 if rec[i][4] else 0.0)
            blev[i] = c + m + (SEM_LAT if succs[i] else 0.0)
        fin = [0.0] * n
        start = [0.0] * n
        rdy = [0.0] * n
        pending = {e: [] for e in ENGS}
        avail = {e: [] for e in ENGS}
        free = {e: 0.0 for e in ENGS}
        for i in range(n):
            if indeg[i] == 0:
                heapq.heappush(pending[rec[i][0]], (0.0, i))
        done = 0
        while done < n:
            best = None
            for e in ENGS:
                pe_, av = pending[e], avail[e]
                while pe_ and pe_[0][0] <= free[e]:
                    j_ = heapq.heappop(pe_)[1]
                    heapq.heappush(av, (-blev[j_], j_))
                if av:
                    ts_ = free[e]
                elif pe_:
                    ts_ = pe_[0][0]
                else:
                    continue
                if best is None or ts_ < best[0]:
                    best = (ts_, e)
            ts_, e = best
            if avail[e]:
                i = heapq.heappop(avail[e])[1]
            else:
                i = heapq.heappop(pending[e])[1]
            eng, fn, reads, writes, dma, dsem, tag, cost = rec[i]
            start[i] = ts_
            free[e] = ts_ + cost
            if dma:
                fin[i] = ts_ + cost + 2000.0
            else:
                fin[i] = ts_ + cost
            done += 1
            for k in succs[i]:
                indeg[k] -= 1
                r = fin[i] + SEM_LAT
                if r > rdy[k]:
                    rdy[k] = r
                if indeg[k] == 0:
                    heapq.heappush(pending[rec[k][0]], (rdy[k], k))
        order = sorted(range(n), key=lambda i: (start[i], i))
        self.sim_span = max(fin) if n else 0.0
        for i in order:
            eng, fn, reads, writes, dma, dsem, tag, cost = rec[i]
            self.tag = tag
            self._op(eng, fn, reads, writes, dma, dsem)

    def _op(self, eng, fn, reads=(), writes=(), dma=False, dma_sem=None):
        idx = len(self.ins[eng])
        deps = []
        for t in reads:
            self._need(eng, t.w, deps)
            if t.excl:
                for ev in list(t.r.values()):
                    if not (ev[0] == "e" and ev[1] == eng):
                        self._need(eng, ev, deps)
        for t in writes:
            self._need(eng, t.w, deps)
            for ev in list(t.r.values()):
                self._need(eng, ev, deps)
        dmainfo = None
        if dma:
            if dma_sem is None:
                if eng == "pool":
                    s = self.n_slot + self.pool_rr
                    self.pool_rr = (self.pool_rr + 1) % 8
                else:
                    s = self.n_slot + 8 + self.sp_rr
                    self.sp_rr = (self.sp_rr + 1) % (self.n_dma_sems - self.n_slot - 8)
            else:
                s = dma_sem
            prev = self.dma_cnt[s]
            if prev > 0:
                self._need(eng, ("d", s, prev * 16), deps)
            self.dma_cnt[s] = prev + 1
            ev = ("d", s, (prev + 1) * 16)
            dmainfo = s
        else:
            ev = ("e", eng, idx)
        self.ins[eng].append([fn, deps, dmainfo])
        self.tags[eng].append(self.tag)
        for t in writes:
            t.w = ev
            t.r = {}
        for t in reads:
            if t in writes:
                continue
            t.r[(ev[0], ev[1])] = ev
        return ev

    def emit(self, es):
        nc = self.nc
        if self.reorder:
            self.schedule()
        else:
            for (eng, fn, reads, writes, dma, dsem, tag, cost) in self.rec:
                self.tag = tag
                self._op(eng, fn, reads, writes, dma, dsem)
        final_events = [("d", s, c * 16) for s, c in enumerate(self.dma_cnt) if c > 0]
        marked = {e: set() for e in ENGS}
        for e in ENGS:
            for fn, deps, dmainfo in self.ins[e]:
                for ev in deps:
                    if ev[0] == "e":
                        marked[ev[1]].add(ev[2])
        cnt = {}
        nsem = {}
        for e in ENGS:
            c = 0
            m = {}
            for i in range(len(self.ins[e])):
                if i in marked[e]:
                    c += 1
                    m[i] = c
            cnt[e] = m
            nsem[e] = (c + SEM_CH - 1) // SEM_CH
        esems = {e: [es.enter_context(nc.semaphore(f"s_{e}{k}")) for k in range(nsem[e])] for e in ENGS}
        dsems = [es.enter_context(nc.semaphore(f"s_dma{k}")) for k in range(self.n_dma_sems)]
        self.stats = {e: (len(self.ins[e]), len(marked[e])) for e in ENGS}

        def do_wait(engobj, ev):
            if ev[0] == "e":
                c = cnt[ev[1]][ev[2]]
                k = (c - 1) // SEM_CH
                engobj.wait_ge(esems[ev[1]][k], c - k * SEM_CH)
            else:
                engobj.wait_ge(dsems[ev[1]], ev[2])

        def section(e, engobj):
            for i, (fn, deps, dmainfo) in enumerate(self.ins[e]):
                for ev in deps:
                    do_wait(engobj, ev)
                h = fn(engobj)
                if dmainfo is not None:
                    h.then_inc(dsems[dmainfo], 16)
                elif i in marked[e]:
                    c = cnt[e][i]
                    k = (c - 1) // SEM_CH
                    h.then_inc(esems[e][k], 1)
            if e == "sp":
                for ev in final_events:
                    do_wait(engobj, ev)

        block = es.enter_context(nc.Block())

        @block.tensor
        def _(eng):
            section("pe", eng)

        @block.vector
        def _(eng):
            section("dve", eng)

        @block.scalar
        def _(eng):
            section("act", eng)

        @block.gpsimd
        def _(eng):
            section("pool", eng)

        @block.sync
        def _(eng):
            section("sp", eng)


def vec_layout(L):
    off = {}
    n = 0

    def add(name, k):
        nonlocal n
        off[name] = n
        n += k

    add("ln_in_g", 8)
    add("ln_in_b", 8)
    for l in range(L):
        for nm, k in (("ln1_g", 8), ("ln1_b", 8), ("ln2_g", 8), ("ln2_b", 8), ("conv_w", 8), ("conv_b", 2),
                      ("b_a", 2), ("b_x", 2), ("lam", 2), ("gn_g", 2), ("gn_b", 2), ("sinks", 4)):
            add(f"{nm}{l}", k)
    return off, n


TAB_F32 = [("dmask_p", 512), ("qdtab_p", 256), ("wend_p", 256),
           ("dmask_s", 512), ("qdtab_s", 256), ("wend_s", 1024)]
TAB_BF = [("avg_bd", 128), ("ones_n", 128), ("ones_pad", 256),
          ("bias_p", 2 * 1024), ("bias_s", 5 * 1024)]


def tab_offsets(tabs):
    off = {}
    n = 0
    for nm, k in tabs:
        off[nm] = n
        n += k
    return off, n


TB = 512
STOP = 99
REORDER = True
DBG = ""


def build(TP, L, NCORES=8):
    NB = TP // TB
    NTB_ = TB // 128
    nc = bass.Bass("TRN2", target_bir_lowering=False)

    def din(name, shape):
        return nc.dram_tensor(name, list(shape), F32, kind="ExternalInput").ap()

    def dout(name, shape):
        return nc.dram_tensor(name, list(shape), F32, kind="ExternalOutput").ap()

    voff, NV = vec_layout(L)
    tfo, NTF = tab_offsets(TAB_F32)
    tbo, NTB = tab_offsets(TAB_BF)

    xpT = din("xpT", [D, TP])
    xsT = din("xsT", [D, 128])
    w_fm = din("w_fm", [L, D, 2048])
    w_tm = din("w_tm", [L, D, 640])
    w_out = din("w_out", [L, D, D])
    w_gate = din("w_gate", [L, D, DFF])
    w_up = din("w_up", [L, D, DFF])
    w_down = din("w_down", [L, DFF, D])
    wrg = din("wrg", [128, L * 4 * 128])
    vecs_d = din("vecs", [128, NV])
    tabf_d = din("tabf", [128, NTF])
    tabb_d = din("tabb", [128, NTB])
    sconv_d = din("sconv", [128, L * 2 * NSQ * 3])
    slru_d = din("slru", [128, L * 2 * NSQ])
    sret_d = din("sret", [128, L * NSQ * 2 * 128])
    sck_d = din("sck", [128, L * NSQ * 2 * 128])
    scv_d = din("scv", [128, L * NSQ * 4 * 128])

    ypT = dout("ypT", [D, TP])
    ysT = dout("ysT", [D, 128])
    o_pconv = dout("o_pconv", [128, L * 2 * 3])
    o_plru = dout("o_plru", [128, L * 2])
    o_pret = dout("o_pret", [128, L * 2 * 128])
    o_pk = dout("o_pk", [128, L * 2 * 128])
    o_pv = dout("o_pv", [128, L * 128])
    o_sconv = dout("o_sconv", [128, L * 2 * NSQ * 3])
    o_slru = dout("o_slru", [128, L * 2 * NSQ])
    o_sret = dout("o_sret", [128, L * NSQ * 2 * 128])
    o_sk = dout("o_sk", [128, L * 2 * 128])
    o_sv = dout("o_sv", [128, L * 128])

    NSLOT = 20
    es = ExitStack()
    with es:
        P = Prog(nc, n_dma_sems=NSLOT + 24, n_slot_sems=NSLOT)
        P.reorder = REORDER

        def sb(name, shape, dt=F32):
            return es.enter_context(nc.sbuf_tensor("sb_" + name, list(shape), dt))

        def toks(name, n):
            return [Tok(f"{name}{i}") for i in range(n)]

        vecs = sb("vecs", [128, NV]); t_vecs = Tok("vecs")
        cvec = sb("cvec", [128, L * 4]); t_cvec = Tok("cvec")
        esink = sb("esink", [128, L * 4]); t_esink = Tok("esink")
        tabf = sb("tabf", [128, NTF]); t_tabf = Tok("tabf")
        tabb = sb("tabb", [128, NTB], BF16); t_tabb = Tok("tabb")
        wrgb = sb("wrgb", [128, L * 4 * 128], BF16); t_wrgb = Tok("wrgb")
        hT = sb("hT", [128, 8, TB]); t_hT = toks("hT", 8)
        hb = sb("hb", [128, 8, TB], BF16); t_hb = toks("hb", 8)
        pre = sb("pre", [128, 8, TB]); t_pre = toks("pre", 8)
        hid = sb("hid", [128, 24, TB], BF16); t_hid = toks("hid", 24)
        preb = hid[:, 0:8, :]; t_preb = t_hid[0:8]
        sqb = hid[:, 8:16, :]; t_sqb = t_hid[8:16]
        qr = hid[:, 0:2, :]; t_qr = t_hid[0:2]
        kr = hid[:, 2:4, :]; t_kr = t_hid[2:4]
        qdec = hid[:, 4:6, :]; t_qdec = t_hid[4:6]
        qa = hid[:, 6:10, :]; t_qa = t_hid[6:10]
        ka = hid[:, 10:12, :]; t_ka = t_hid[10:12]
        ubf = hid[:, 12:14, :]; t_ub = t_hid[12:14]
        oT = hid[:, 16:24, :]; t_oT = t_hid[16:24]
        mean_sb = sb("mean_sb", [128, TB]); t_mean = Tok("mean")
        rstd_sb = sb("rstd_sb", [128, TB]); t_rstd = Tok("rstd")
        axbuf = sb("axbuf", [128, 2, TB + 3]); t_ax = toks("ax", 2)
        u32 = sb("u32", [128, 2, TB]); t_u = toks("u", 2)
        rr = pre[:, 0:2, :]; t_rr = t_pre[0:2]
        gi = pre[:, 2:4, :]; t_gi = t_pre[2:4]
        aa = pre[:, 4:6, :]; t_aa = t_pre[4:6]
        bt = pre[:, 6:8, :]; t_bt = t_pre[6:8]
        hh = sb("hh", [128, 2, TB]); t_hh = toks("hh", 2)
        gag = sb("gag", [128, 2, TB]); t_gag = toks("gag", 2)
        sbg = sb("sbg", [128, 2, TB]); t_sbg = toks("sbg", 2)
        kdec = sb("kdec", [128, 4, 256], BF16); t_kdec = toks("kdec", 4)
        vr = sb("vr", [128, NTB_, 256], BF16); t_vr = toks("vr", NTB_)
        vrp = sb("vrp", [128, NTB_, 4, 128], BF16); t_vrp = toks("vrp", NTB_)
        vap = sb("vap", [128, NTB_, 2, 2, 128], BF16); t_vap = toks("vap", NTB_)
        pexp = sb("pexp", [128, 2, 8, 128], BF16); t_pexp = toks("pexp", 2)
        xsc = sb("xsc", [128, 2, 8, 128]); t_xsc = toks("xsc", 2)
        pret = sb("pret", [128, 2, 4, 128], BF16); t_pret = toks("pret", 2)
        ro = sb("ro", [128, 2, TB]); t_ro = toks("ro", 2)
        rob = sb("rob", [128, 2, TB], BF16); t_rob = toks("rob", 2)
        rosq = sb("rosq", [128, 2, TB], BF16); t_rosq = toks("rosq", 2)
        tmpa = mean_sb; t_tmpa = t_mean
        tmpb = rstd_sb; t_tmpb = t_rstd
        rden = sb("rden", [128, 4, 128]); t_rden = Tok("rden")
        sgt = pre[:, 0:4, :]; t_sgt = t_pre[0:4]
        wpool = sb("wpool", [128, NSLOT, 640], BF16); t_w = toks("w", NSLOT)
        ctail = sb("ctail", [128, L, 2, 3]); t_ctail = toks("ctail", L)
        hst = sb("hst", [128, L, 2]); t_hst = toks("hst", L)
        S32 = sb("S32", [128, L, 2, 128]); t_S32 = toks("S32", L)
        Sbf = sb("Sbf", [128, L, 2, 128], BF16); t_Sbf = toks("Sbf", L)
        khalo = sb("khalo", [128, L, 2, 128], BF16); t_khalo = toks("khalo", L)
        vhalo = sb("vhalo", [128, L, 2, 2, 128], BF16); t_vhalo = toks("vhalo", L)
        kouts = sb("kouts", [128, 1, 2, 128]); t_kouts = toks("kouts", 1)
        vouts = sb("vouts", [128, 1, 128]); t_vouts = toks("vouts", 1)
        s_ctail = sb("s_ctail", [128, L, 2, NSQ, 3]); t_sct = toks("sct", L)
        s_hst = sb("s_hst", [128, L, 2, NSQ]); t_shst = toks("shst", L)
        s_S32 = sb("s_S32", [128, 1, NSQ, 2, 128]); t_sS32 = toks("sS32", 1)
        s_Sbf = sb("s_Sbf", [128, 1, NSQ, 2, 128], BF16); t_sSbf = toks("sSbf", 1)
        s_ck = sb("s_ck", [128, 1, NSQ, 2, 128], BF16); t_sck = toks("sck", 1)
        s_cv = sb("s_cv", [128, 1, NSQ, 2, 2, 128], BF16); t_scv = toks("scv", 1)
        s_kouts = kouts; t_skouts = t_kouts
        s_vouts = vouts; t_svouts = t_vouts

        psum = es.enter_context(nc.psum_tensor("psum", [128, 8, 512], F32))
        t_ps = [Tok(f"ps{i}", excl=True) for i in range(8)]
        ps_rr = [0]

        held = set()

        def bank(hold=False):
            while ps_rr[0] in held:
                ps_rr[0] = (ps_rr[0] + 1) % 8
            b = ps_rr[0]
            ps_rr[0] = (b + 1) % 8
            if hold:
                held.add(b)
            return b

        def nfree(ap):
            n_ = 1
            for d_ in ap.shape[1:]:
                n_ *= int(d_)
            return n_

        def act(out, in_, func, reads, writes, **kw):
            P.op("act", lambda e: e.activation(out=out, in_=in_, func=func, **kw), reads, writes, cost=420.0 + nfree(out) / 1.0)

        def tt(out, in0, in1, op, reads, writes):
            P.op("dve", lambda e: e.tensor_tensor(out=out, in0=in0, in1=in1, op=op), reads, writes, cost=260.0 + nfree(out) / 0.9)

        def ts(out, in0, s1, s2, op0, op1, reads, writes):
            if op1 is None:
                P.op("dve", lambda e: e.tensor_scalar(out=out, in0=in0, scalar1=s1, scalar2=None, op0=op0), reads, writes, cost=260.0 + nfree(out) / 0.9)
            else:
                P.op("dve", lambda e: e.tensor_scalar(out=out, in0=in0, scalar1=s1, scalar2=s2, op0=op0, op1=op1), reads, writes, cost=260.0 + nfree(out) / 0.9)

        def stt(out, in0, scalar, in1, op0, op1, reads, writes):
            P.op("dve", lambda e: e.scalar_tensor_tensor(out=out, in0=in0, scalar=scalar, in1=in1, op0=op0, op1=op1), reads, writes, cost=260.0 + nfree(out) / 0.9)

        def dcopy(out, in_, reads, writes):
            P.op("dve", lambda e: e.tensor_copy(out=out, in_=in_), reads, writes, cost=260.0 + nfree(out) / 0.9)

        def mm(out, lhsT, rhs, start, stop, reads, writes, skip=False):
            P.op("pe", lambda e: e.matmul(out, lhsT=lhsT, rhs=rhs, start=start, stop=stop, skip_group_check=skip), reads, writes,
                 cost=25.0 + max(107.0, nfree(rhs) / 2.4))

        def dma(eng, out, in_, reads, writes, sem=None):
            return P.op(eng, lambda e: e.dma_start(out=out, in_=in_), reads, writes, dma=True, dma_sem=sem)

        slot_rr = [0]

        def wload(src_ap, ncols):
            s = slot_rr[0]
            slot_rr[0] = (s + 1) % NSLOT
            dma("pool", wpool[:, s, 0:ncols], src_ap, [], [t_w[s]], sem=s)
            return s

        def vcol(name, j=0, n=1):
            o = voff[name] + j
            return vecs[:, o:o + n]

        dma("sp", vecs[:], vecs_d, [], [t_vecs])
        dma("sp", tabf[:], tabf_d, [], [t_tabf])
        dma("pool", tabb[:], tabb_d, [], [t_tabb])
        dma("pool", wrgb[:], wrg, [], [t_wrgb])
        dma("sp", s_ctail[:].rearrange("p l c s j -> p (l c s j)"), sconv_d, [], t_sct)
        dma("sp", s_hst[:].rearrange("p l c s -> p (l c s)"), slru_d, [], t_shst)
        P.op("dve", lambda e: e.memset(vrp[:], 0.0), [], t_vrp)
        P.op("dve", lambda e: e.memset(vap[:], 0.0), [], t_vap)
        P.op("dve", lambda e: e.memset(ctail[:], 0.0), [], t_ctail)
        P.op("dve", lambda e: e.memset(hst[:], 0.0), [], t_hst)
        P.op("dve", lambda e: e.memset(S32[:], 0.0), [], t_S32)
        P.op("dve", lambda e: e.memset(Sbf[:], 0.0), [], t_Sbf)
        for l in range(L):
            lam = vcol(f"lam{l}", 0, 2)
            c1 = cvec[:, l * 4:l * 4 + 2]
            c2 = cvec[:, l * 4 + 2:l * 4 + 4]
            act(c1, lam, AF.Exp, [t_vecs], [t_cvec], scale=-1.0)
            act(c1, c1, AF.Ln, [t_cvec], [t_cvec], bias=1.0, scale=1.0)
            ts(c2, c1, -16.0, None, ALU.mult, None, [t_cvec], [t_cvec])
            ts(c1, c1, -8.0, None, ALU.mult, None, [t_cvec], [t_cvec])
            act(esink[:, l * 4:l * 4 + 4], vcol(f"sinks{l}", 0, 4), AF.Exp, [t_vecs], [t_esink])

        def layer_norm(T, gname, bname):
            ones_n = tabb[:, tbo["ones_n"]:tbo["ones_n"] + 128]
            for k in range(8):
                dcopy(preb[:, k, :T], pre[:, k, :T], [t_pre[k]], [t_preb[k]])
                act(sqb[:, k, :T], pre[:, k, :T], AF.Square, [t_pre[k]], [t_sqb[k]])
            bm = bank()
            for k in range(8):
                mm(psum[:, bm, :T], ones_n, preb[:, k, :T], k == 0, k == 7, [t_tabb, t_preb[k]], [t_ps[bm]])
            bq = bank()
            for k in range(8):
                mm(psum[:, bq, :T], ones_n, sqb[:, k, :T], k == 0, k == 7, [t_tabb, t_sqb[k]], [t_ps[bq]])
            act(mean_sb[:, :T], psum[:, bm, :T], AF.Copy, [t_ps[bm]], [t_mean])
            act(rstd_sb[:, :T], psum[:, bm, :T], AF.Square, [t_ps[bm]], [t_rstd])
            tt(rstd_sb[:, :T], psum[:, bq, :T], rstd_sb[:, :T], ALU.subtract, [t_ps[bq], t_rstd], [t_rstd])
            act(rstd_sb[:, :T], rstd_sb[:, :T], AF.Sqrt, [t_rstd], [t_rstd], bias=EPS, scale=1.0)
            P.op("dve", lambda e: e.reciprocal(out=rstd_sb[:, :T], in_=rstd_sb[:, :T]), [t_rstd], [t_rstd])
            for k in range(8):
                tt(pre[:, k, :T], pre[:, k, :T], mean_sb[:, :T], ALU.subtract, [t_pre[k], t_mean], [t_pre[k]])
                tt(pre[:, k, :T], pre[:, k, :T], rstd_sb[:, :T], ALU.mult, [t_pre[k], t_rstd], [t_pre[k]])
                act(hb[:, k, :T], pre[:, k, :T], AF.Identity, [t_pre[k], t_vecs], [t_hb[k]],
                    scale=vcol(gname, k), bias=vcol(bname, k))
                act(hT[:, k, :T], pre[:, k, :T], AF.Identity, [t_pre[k], t_vecs], [t_hT[k]],
                    scale=vcol(gname, k), bias=vcol(bname, k))

        def block_layer(kind, l, T, first, last):
            NT = T // 128
            samp = kind == "s"
            if samp:
                n1 = NSQ * 2 * 128
                dma("sp", s_S32[:].rearrange("p l s c e -> p (l s c e)"), sret_d[:, l * n1:(l + 1) * n1], [], t_sS32)
                dma("pool", s_Sbf[:].rearrange("p l s c e -> p (l s c e)"), sret_d[:, l * n1:(l + 1) * n1], [], t_sSbf)
                dma("pool", s_ck[:].rearrange("p l s c e -> p (l s c e)"), sck_d[:, l * n1:(l + 1) * n1], [], t_sck)
                dma("pool", s_cv[:].rearrange("p l s c h e -> p (l s c h e)"), scv_d[:, l * 2 * n1:(l + 1) * 2 * n1], [], t_scv)
            if STOP < 2:
                return
            P.tag = "inproj_fm"
            for e in range(16):
                if e % 4 == 0:
                    gb = [bank(hold=True) for _ in range(4)]
                    for k in range(8):
                        sl = wload(w_fm[l, k * 128:(k + 1) * 128, e * 128:(e + 4) * 128], 512)
                        for j in range(4):
                            mm(psum[:, gb[j], :T], wpool[:, sl, j * 128:(j + 1) * 128], hb[:, k, :T], k == 0, k == 7,
                               [t_w[sl], t_hb[k]], [t_ps[gb[j]]])
                    for j in range(4):
                        held.discard(gb[j])
                b = gb[e % 4]
                src = psum[:, b, :T]
                if e < 2:
                    if samp:
                        dst = axbuf[:, e, 0:NSQ * 35].rearrange("p (s j) -> p s j", j=35)[:, :, 3:35]
                        srcv = src.rearrange("p (s j) -> p s j", j=ST)
                    else:
                        dst = axbuf[:, e, 3:3 + T]
                        srcv = src
                    act(dst, srcv, AF.Copy, [t_ps[b]], [t_ax[e]])
                elif e < 4:
                    act(gag[:, e - 2, :T], src, AF.Gelu_apprx_tanh, [t_ps[b]], [t_gag[e - 2]])
                elif e < 6:
                    pr = e - 4
                    if not DBG.endswith("b"):
                        act(qr[:, pr, :T], src, AF.Copy, [t_ps[b]], [t_qr[pr]])
                    tname = "qdtab_s" if samp else "qdtab_p"
                    tab = tabf[:, tfo[tname] + pr * 128: tfo[tname] + (pr + 1) * 128]
                    for t_ in range(NT):
                        if DBG.endswith("a"):
                            continue
                        tt(qdec[:, pr, t_ * 128:(t_ + 1) * 128], psum[:, b, t_ * 128:(t_ + 1) * 128], tab, ALU.mult,
                           [t_ps[b], t_tabf] + ([t_qr[pr]] if DBG.endswith("c") else []), [t_qdec[pr]])
                elif e < 8:
                    act(kr[:, e - 6, :T], src, AF.Copy, [t_ps[b]], [t_kr[e - 6]])
                elif e < 10:
                    act(sbg[:, e - 8, :T], src, AF.Silu, [t_ps[b]], [t_sbg[e - 8]])
                elif e < 14:
                    if e % 2 == 0:
                        act(qa[:, e - 10, :T], src, AF.Copy, [t_ps[b]], [t_qa[e - 10]])
                    else:
                        dcopy(qa[:, e - 10, :T], src, [t_ps[b]], [t_qa[e - 10]])
                else:
                    hk = e - 14
                    dcopy(ka[:, hk, :T], src, [t_ps[b]], [t_ka[hk]])
                    if samp:
                        act(s_kouts[:, 0, hk, :], src, AF.Copy, [t_ps[b]], [t_skouts[0]])
                        if hk == 1:
                            dma("sp", o_sk[:, l * 256:(l + 1) * 256], kouts[:].rearrange("p l c e -> p (l c e)"), t_kouts, [])
                    elif last:
                        act(kouts[:, 0, hk, :], psum[:, b, T - 128:T], AF.Copy, [t_ps[b]], [t_kouts[0]])
                        if hk == 1:
                            dma("sp", o_pk[:, l * 256:(l + 1) * 256], kouts[:].rearrange("p l c e -> p (l c e)"), t_kouts, [])
            if STOP < 3:
                return
            P.tag = "inproj_tm"
            wend_off = tfo["wend_s"] if samp else tfo["wend_p"]
            bAs = [bank(hold=True) for _ in range(NT)]
            bBs = bank(hold=True)
            for k in range(8):
                sl = wload(w_tm[l, k * 128:(k + 1) * 128, :], 640)
                for t in range(NT):
                    mm(psum[:, bAs[t], 0:512], hb[:, k, t * 128:(t + 1) * 128], wpool[:, sl, 0:512], k == 0, k == 7,
                       [t_w[sl], t_hb[k]], [t_ps[bAs[t]]])
                    mm(psum[:, bBs, t * 128:(t + 1) * 128], hb[:, k, t * 128:(t + 1) * 128], wpool[:, sl, 512:640], k == 0 and t == 0, k == 7,
                       [t_w[sl], t_hb[k]], [t_ps[bBs]], skip=True)
            for b_ in bAs:
                held.discard(b_)
            held.discard(bBs)
            for t in range(NT):
                bA = bAs[t]
                if samp:
                    for g in range(NSQ):
                        tt(kdec[:, g, :], psum[:, bA, 0:256], tabf[:, wend_off + g * 256: wend_off + (g + 1) * 256], ALU.mult,
                           [t_ps[bA], t_tabf], [t_kdec[g]])
                else:
                    tt(kdec[:, t, :], psum[:, bA, 0:256], tabf[:, wend_off: wend_off + 256], ALU.mult,
                       [t_ps[bA], t_tabf], [t_kdec[t]])
                act(vr[:, t, :], psum[:, bA, 256:512], AF.Copy, [t_ps[bA]], [t_vr[t]])
                vsrc = psum[:, bA, 256:512].rearrange("p (c h e) -> p c h e", c=2, h=2)
                dcopy(vrp[:, t, 0:4:2, 0:64], vsrc[:, :, 0, :], [t_ps[bA]], [t_vrp[t]])
                dcopy(vrp[:, t, 1:4:2, 64:128], vsrc[:, :, 1, :], [t_ps[bA]], [t_vrp[t]])
                csrc = psum[:, bBs, t * 128:(t + 1) * 128].rearrange("p (h e) -> p h e", h=2)
                dcopy(vap[:, t, :, 0, 0:64], csrc, [t_ps[bBs]], [t_vap[t]])
                dcopy(vap[:, t, :, 1, 64:128], csrc, [t_ps[bBs]], [t_vap[t]])
                if samp:
                    act(s_vouts[:, 0, :], psum[:, bBs, t * 128:(t + 1) * 128], AF.Copy, [t_ps[bBs]], [t_svouts[0]])
                    dma("sp", o_sv[:, l * 128:(l + 1) * 128], vouts[:, 0, :], t_vouts, [])
                elif last and t == NT - 1:
                    act(vouts[:, 0, :], psum[:, bBs, t * 128:(t + 1) * 128], AF.Copy, [t_ps[bBs]], [t_vouts[0]])
                    dma("sp", o_pv[:, l * 128:(l + 1) * 128], vouts[:, 0, :], t_vouts, [])

            if STOP < 4:
                return
            def lru_chunk(ch):
                P.tag = "lru"
                if samp:
                    axv = axbuf[:, ch, 0:NSQ * 35].rearrange("p (s j) -> p s j", j=35)
                    dcopy(axv[:, :, 0:3], s_ctail[:, l, ch, :, :], [t_sct[l]], [t_ax[ch]])
                    sh = lambda j: axv[:, :, j:j + ST]
                    uv = u32[:, ch, :T].rearrange("p (s j) -> p s j", j=ST)
                else:
                    dcopy(axbuf[:, ch, 0:3], ctail[:, l, ch, :], [t_ctail[l]], [t_ax[ch]])
                    sh = lambda j: axbuf[:, ch, j:j + T]
                    uv = u32[:, ch, :T]
                cw = lambda j: vcol(f"conv_w{l}", j * 2 + ch)
                act(uv, sh(3), AF.Identity, [t_ax[ch], t_vecs], [t_u[ch]], scale=cw(3), bias=vcol(f"conv_b{l}", ch))
                for j in (2, 1, 0):
                    stt(uv, sh(j), cw(j), uv, ALU.mult, ALU.add, [t_ax[ch], t_vecs, t_u[ch]], [t_u[ch]])
                if samp:
                    dcopy(s_ctail[:, l, ch, :, :], axv[:, :, ST:ST + 3], [t_ax[ch]], [t_sct[l]])
                else:
                    dcopy(ctail[:, l, ch, :], axbuf[:, ch, T:T + 3], [t_ax[ch]], [t_ctail[l]])
                act(ubf[:, ch, :T], u32[:, ch, :T], AF.Copy, [t_u[ch]], [t_ub[ch]])
                ba = bank()
                mm(psum[:, ba, :T], wrgb[:, (l * 4 + ch) * 128:(l * 4 + ch + 1) * 128], ubf[:, ch, :T], True, True,
                   [t_wrgb, t_ub[ch]], [t_ps[ba]])
                bx = bank()
                mm(psum[:, bx, :T], wrgb[:, (l * 4 + 2 + ch) * 128:(l * 4 + 3 + ch) * 128], ubf[:, ch, :T], True, True,
                   [t_wrgb, t_ub[ch]], [t_ps[bx]])
                act(rr[:, ch, :T], psum[:, ba, :T], AF.Sigmoid, [t_ps[ba], t_vecs], [t_rr[ch]], bias=vcol(f"b_a{l}", ch), scale=1.0)
                act(gi[:, ch, :T], psum[:, bx, :T], AF.Sigmoid, [t_ps[bx], t_vecs], [t_gi[ch]], bias=vcol(f"b_x{l}", ch), scale=1.0)
                act(aa[:, ch, :T], rr[:, ch, :T], AF.Exp, [t_rr[ch], t_cvec], [t_aa[ch]], scale=cvec[:, l * 4 + ch:l * 4 + ch + 1])
                act(bt[:, ch, :T], rr[:, ch, :T], AF.Exp, [t_rr[ch], t_cvec], [t_bt[ch]], scale=cvec[:, l * 4 + 2 + ch:l * 4 + 3 + ch])
                ts(bt[:, ch, :T], bt[:, ch, :T], -1.0, 1.0, ALU.mult, ALU.add, [t_bt[ch]], [t_bt[ch]])
                act(bt[:, ch, :T], bt[:, ch, :T], AF.Sqrt, [t_bt[ch]], [t_bt[ch]])
                tt(gi[:, ch, :T], gi[:, ch, :T], u32[:, ch, :T], ALU.mult, [t_gi[ch], t_u[ch]], [t_gi[ch]])
                tt(bt[:, ch, :T], bt[:, ch, :T], gi[:, ch, :T], ALU.mult, [t_bt[ch], t_gi[ch]], [t_bt[ch]])
                if samp:
                    for s_ in range(NSQ):
                        c0 = s_ * ST
                        P.op("dve", lambda e, c0=c0, s_=s_, ch=ch: e.tensor_tensor_scan(
                            out=hh[:, ch, c0:c0 + ST], data0=aa[:, ch, c0:c0 + ST], data1=bt[:, ch, c0:c0 + ST],
                            initial=s_hst[:, l, ch, s_:s_ + 1], op0=ALU.mult, op1=ALU.add),
                            [t_aa[ch], t_bt[ch], t_shst[l]], [t_hh[ch]], cost=260.0)
                    dcopy(s_hst[:, l, ch, :], hh[:, ch, :T].rearrange("p (s j) -> p s j", j=ST)[:, :, ST - 1], [t_hh[ch]], [t_shst[l]])
                else:
                    P.op("dve", lambda e, ch=ch: e.tensor_tensor_scan(
                        out=hh[:, ch, :T], data0=aa[:, ch, :T], data1=bt[:, ch, :T],
                        initial=hst[:, l, ch:ch + 1], op0=ALU.mult, op1=ALU.add),
                        [t_aa[ch], t_bt[ch], t_hst[l]], [t_hh[ch]], cost=150.0 + 2.1 * T)
                    dcopy(hst[:, l, ch:ch + 1], hh[:, ch, T - 1:T], [t_hh[ch]], [t_hst[l]])
                tt(oT[:, ch, :T], hh[:, ch, :T], gag[:, ch, :T], ALU.mult, [t_hh[ch], t_gag[ch]], [t_oT[ch]])

            dm_off = tfo["dmask_s"] if samp else tfo["dmask_p"]
            dmask = tabf[:, dm_off:dm_off + 512].rearrange("p (h i) -> p h i", h=4)
            gdec = [(1.0 - 2.0 ** (-5.0 - h)) ** (ST if samp else 128) for h in range(4)]
            def ret_tile(t):
                P.tag = "ret"
                cs = slice(t * 128, (t + 1) * 128)
                bsl = [bank(), bank()]
                for h in range(4):
                    pr, hf = divmod(h, 2)
                    rows = slice(hf * 64, hf * 64 + 64)
                    mm(psum[:, bsl[hf], pr * 128:(pr + 1) * 128], kr[rows, pr, cs], qr[rows, pr, cs], True, True,
                       [t_kr[pr], t_qr[pr]], [t_ps[bsl[hf]]])
                pb = t % 2
                for hf in range(2):
                    tt(pret[:, pb, hf:4:2, :], psum[:, bsl[hf], 0:256].rearrange("p (h i) -> p h i", h=2), dmask[:, hf:4:2, :], ALU.mult,
                       [t_ps[bsl[hf]], t_tabf], [t_pret[pb]])
                bo = bank()
                for pr in range(2):
                    oc = slice(pr * 128, (pr + 1) * 128)
                    for hf in range(2):
                        h = pr * 2 + hf
                        mm(psum[:, bo, oc], vrp[:, t, h, :], pret[:, pb, h, :], hf == 0, False,
                           [t_vrp[t], t_pret[pb]], [t_ps[bo]])
                    if samp:
                        for g in range(NSQ):
                            gc = slice(pr * 128 + g * ST, pr * 128 + (g + 1) * ST)
                            mm(psum[:, bo, gc], s_Sbf[:, 0, g, pr, :], qdec[:, pr, g * ST:(g + 1) * ST], False, g == NSQ - 1,
                               [t_sSbf[0], t_qdec[pr]], [t_ps[bo]])
                    else:
                        mm(psum[:, bo, oc], Sbf[:, l, pr, :], qdec[:, pr, cs], False, True,
                           [t_Sbf[l], t_qdec[pr]], [t_ps[bo]])
                act(ro[:, :, cs], psum[:, bo, 0:256].rearrange("p (c i) -> p c i", c=2), AF.Copy, [t_ps[bo]], t_ro)
                ngrp = NSQ if samp else 1
                for g in range(ngrp):
                    bu = bank()
                    kd = kdec[:, g, :] if samp else kdec[:, t, :]
                    for pr in range(2):
                        mm(psum[:, bu, pr * 128:(pr + 1) * 128], kd[:, pr * 128:(pr + 1) * 128], vr[:, t, pr * 128:(pr + 1) * 128],
                           True, True, [t_kdec[g if samp else t], t_vr[t]], [t_ps[bu]])
                    for h in range(4):
                        if DBG == "r3":
                            continue
                        pr, hf = divmod(h, 2)
                        rows = slice(hf * 64, hf * 64 + 64)
                        cols = slice(hf * 64, hf * 64 + 64)
                        if samp:
                            S_ = s_S32[rows, 0, g, pr, cols]
                            Sb_ = s_Sbf[rows, 0, g, pr, cols]
                            tS, tSb = t_sS32[0], t_sSbf[0]
                        else:
                            S_ = S32[rows, l, pr, cols]
                            Sb_ = Sbf[rows, l, pr, cols]
                            tS, tSb = t_S32[l], t_Sbf[l]
                        stt(S_, S_, float(gdec[h]), psum[rows, bu, pr * 128 + hf * 64: pr * 128 + hf * 64 + 64], ALU.mult, ALU.add,
                            [tS, t_ps[bu]], [tS])
                        act(Sb_, S_, AF.Copy, [tS], [tSb])
                if samp:
                    n1 = NSQ * 2 * 128
                    dma("sp", o_sret[:, l * n1:(l + 1) * n1], s_S32[:].rearrange("p l s c e -> p (l s c e)"), t_sS32, [])
            avg = tabb[:, tbo["avg_bd"]:tbo["avg_bd"] + 128]
            def headnorm(c):
                P.tag = "ret"
                act(rob[:, c, :T], ro[:, c, :T], AF.Copy, [t_ro[c]], [t_rob[c]])
                act(rosq[:, c, :T], ro[:, c, :T], AF.Square, [t_ro[c]], [t_rosq[c]])
                bm = bank()
                mm(psum[:, bm, :T], avg, rob[:, c, :T], True, True, [t_tabb, t_rob[c]], [t_ps[bm]])
                bq = bank()
                mm(psum[:, bq, :T], avg, rosq[:, c, :T], True, True, [t_tabb, t_rosq[c]], [t_ps[bq]])
                act(tmpa[:, :T], psum[:, bm, :T], AF.Square, [t_ps[bm]], [t_tmpa])
                tt(tmpa[:, :T], psum[:, bq, :T], tmpa[:, :T], ALU.subtract, [t_ps[bq], t_tmpa], [t_tmpa])
                act(tmpa[:, :T], tmpa[:, :T], AF.Sqrt, [t_tmpa], [t_tmpa], bias=EPS, scale=1.0)
                P.op("dve", lambda e: e.reciprocal(out=tmpa[:, :T], in_=tmpa[:, :T]), [t_tmpa], [t_tmpa])
                tt(tmpb[:, :T], ro[:, c, :T], psum[:, bm, :T], ALU.subtract, [t_ro[c], t_ps[bm]], [t_tmpb])
                tt(tmpb[:, :T], tmpb[:, :T], tmpa[:, :T], ALU.mult, [t_tmpb, t_tmpa], [t_tmpb])
                act(tmpb[:, :T], tmpb[:, :T], AF.Identity, [t_tmpb, t_vecs], [t_tmpb],
                    scale=vcol(f"gn_g{l}", c), bias=vcol(f"gn_b{l}", c))
                tt(oT[:, 2 + c, :T], tmpb[:, :T], sbg[:, c, :T], ALU.mult, [t_tmpb, t_sbg[c]], [t_oT[2 + c]])

            ones_pad = tabb[:, tbo["ones_pad"]:tbo["ones_pad"] + 256].rearrange("p (h e) -> p h e", h=2)
            def attn_tile(t):
                P.tag = "attn"
                cs = slice(t * 128, (t + 1) * 128)
                ktiles = []
                if samp:
                    for g in range(NSQ):
                        bo_ = tbo["bias_s"] + g * 1024
                        ktiles.append((lambda hk, rows, g=g: s_ck[rows, 0, g, hk, :], lambda hk, hf, g=g: s_cv[:, 0, g, hk, hf, :],
                                       tabb[:, bo_:bo_ + 1024], [t_sck[0]], [t_scv[0]]))
                    bo_ = tbo["bias_s"] + 4 * 1024
                    ktiles.append((lambda hk, rows: ka[rows, hk, 0:128], lambda hk, hf: vap[:, 0, hk, hf, :],
                                   tabb[:, bo_:bo_ + 1024], t_ka, [t_vap[0]]))
                else:
                    bo_ = tbo["bias_p"]
                    if t > 0:
                        ktiles.append((lambda hk, rows, t=t: ka[rows, hk, (t - 1) * 128:t * 128], lambda hk, hf, t=t: vap[:, t - 1, hk, hf, :],
                                       tabb[:, bo_:bo_ + 1024], t_ka, [t_vap[t - 1]]))
                    elif not first:
                        ktiles.append((lambda hk, rows: khalo[rows, l, hk, :], lambda hk, hf: vhalo[:, l, hk, hf, :],
                                       tabb[:, bo_:bo_ + 1024], [t_khalo[l]], [t_vhalo[l]]))
                    ktiles.append((lambda hk, rows, t=t: ka[rows, hk, t * 128:(t + 1) * 128], lambda hk, hf, t=t: vap[:, t, hk, hf, :],
                                   tabb[:, bo_ + 1024:bo_ + 2048], t_ka, [t_vap[t]]))
                bn = bank(hold=True)
                bd = bank(hold=True)
                nk = len(ktiles)
                for ki, (kfn, vfn, btab, ktk, vtk) in enumerate(ktiles):
                    pb = (t * 5 + ki) % 2
                    bb = [bank(), bank()]
                    for hq in range(8):
                        cq, hf = divmod(hq, 2)
                        hk = hq // 4
                        rows = slice(hf * 64, hf * 64 + 64)
                        mm(psum[:, bb[hf], cq * 128:(cq + 1) * 128], kfn(hk, rows), qa[rows, cq, cs], True, True,
                           list(ktk) + [t_qa[cq]], [t_ps[bb[hf]]])
                    btab3 = btab.rearrange("p (h q) -> p h q", h=8)
                    for hf in range(2):
                        tt(xsc[:, pb, hf:8:2, :], psum[:, bb[hf], :].rearrange("p (h q) -> p h q", h=4),
                           btab3[:, hf:8:2, :], ALU.add, [t_ps[bb[hf]], t_tabb], [t_xsc[pb]])
                    act(pexp[:, pb, :, :], xsc[:, pb, :, :], AF.Exp, [t_xsc[pb]], [t_pexp[pb]], scale=0.125)
                    for cq in range(4):
                        oc = slice(cq * 128, (cq + 1) * 128)
                        for hf in range(2):
                            hq = cq * 2 + hf
                            hk = hq // 4
                            st_ = ki == 0 and hf == 0 and cq == 0
                            sp_ = ki == nk - 1 and hf == 1
                            mm(psum[:, bn, oc], vfn(hk, hf), pexp[:, pb, hq, :], st_, sp_,
                               list(vtk) + [t_pexp[pb]], [t_ps[bn]], skip=True)
                            mm(psum[:, bd, oc], ones_pad[:, hf, :], pexp[:, pb, hq, :], st_, sp_,
                               [t_tabb, t_pexp[pb]], [t_ps[bd]], skip=True)
                held.discard(bn)
                held.discard(bd)
                for cq in range(4):
                    ts(rden[:, cq, :], psum[:, bd, cq * 128:(cq + 1) * 128], esink[:, l * 4 + cq:l * 4 + cq + 1], None, ALU.add, None,
                       [t_ps[bd], t_esink], [t_rden])
                P.op("dve", lambda e: e.reciprocal(out=rden[:], in_=rden[:]), [t_rden], [t_rden])
                tt(oT[:, 4:8, cs], psum[:, bn, :].rearrange("p (c q) -> p c q", c=4), rden[:], ALU.mult,
                   [t_ps[bn], t_rden], t_oT[4:8])
            for t in range(max(NT, 2)):
                if t < NT:
                    attn_tile(t)
                if t < 2:
                    lru_chunk(t)
                if t < NT:
                    ret_tile(t)
            headnorm(0)
            headnorm(1)
            if not samp and not last:
                dcopy(khalo[:, l, :, :], ka[:, :, T - 128:T], t_ka, [t_khalo[l]])
                act(vhalo[:, l, :, :, :], vap[:, NT - 1, :, :, :], AF.Copy, [t_vap[NT - 1]], [t_vhalo[l]])

            if STOP < 7:
                return
            P.tag = "outproj_ln1"
            for g2 in range(2):
                gb = [bank(hold=True) for _ in range(4)]
                for k in range(8):
                    sl = wload(w_out[l, k * 128:(k + 1) * 128, g2 * 512:(g2 + 1) * 512], 512)
                    for j in range(4):
                        mm(psum[:, gb[j], :T], wpool[:, sl, j * 128:(j + 1) * 128], oT[:, k, :T], k == 0, k == 7,
                           [t_w[sl], t_oT[k]], [t_ps[gb[j]]])
                for j in range(4):
                    held.discard(gb[j])
                    dc = g2 * 4 + j
                    stt(pre[:, dc, :T], hT[:, dc, :T], float(ALPHA), psum[:, gb[j], :T], ALU.mult, ALU.add, [t_hT[dc], t_ps[gb[j]]], [t_pre[dc]])
            layer_norm(T, f"ln1_g{l}", f"ln1_b{l}")

            if STOP < 8:
                return
            P.tag = "ffn_gu"
            for f0 in range(0, NFF, 4):
                nf = min(4, NFF - f0)
                gbk = [bank(hold=True) for _ in range(nf)]
                for k in range(8):
                    sl = wload(w_gate[l, k * 128:(k + 1) * 128, f0 * 128:(f0 + nf) * 128], nf * 128)
                    for j in range(nf):
                        mm(psum[:, gbk[j], :T], wpool[:, sl, j * 128:(j + 1) * 128], hb[:, k, :T], k == 0, k == 7,
                           [t_w[sl], t_hb[k]], [t_ps[gbk[j]]])
                for j in range(nf):
                    held.discard(gbk[j])
                    act(sgt[:, j, :T], psum[:, gbk[j], :T], AF.Silu, [t_ps[gbk[j]]], [t_sgt[j]])
                ubk = [bank(hold=True) for _ in range(nf)]
                for k in range(8):
                    sl = wload(w_up[l, k * 128:(k + 1) * 128, f0 * 128:(f0 + nf) * 128], nf * 128)
                    for j in range(nf):
                        mm(psum[:, ubk[j], :T], wpool[:, sl, j * 128:(j + 1) * 128], hb[:, k, :T], k == 0, k == 7,
                           [t_w[sl], t_hb[k]], [t_ps[ubk[j]]])
                for j in range(nf):
                    held.discard(ubk[j])
                    tt(hid[:, f0 + j, :T], sgt[:, j, :T], psum[:, ubk[j], :T], ALU.mult, [t_sgt[j], t_ps[ubk[j]]], [t_hid[f0 + j]])
            P.tag = "ffn_down"
            for half in range(2):
                banks = [bank() for _ in range(4)]
                for k in range(NFF):
                    s = wload(w_down[l, k * 128:(k + 1) * 128, half * 512:(half + 1) * 512], 512)
                    for j in range(4):
                        mm(psum[:, banks[j], :T], wpool[:, s, j * 128:(j + 1) * 128], hid[:, k, :T], k == 0, k == NFF - 1,
                           [t_w[s], t_hid[k]], [t_ps[banks[j]]])
                for j in range(4):
                    dc = half * 4 + j
                    stt(pre[:, dc, :T], hT[:, dc, :T], float(ALPHA), psum[:, banks[j], :T], ALU.mult, ALU.add,
                        [t_hT[dc], t_ps[banks[j]]], [t_pre[dc]])
            P.tag = "ln2"
            layer_norm(T, f"ln2_g{l}", f"ln2_b{l}")

        def run_block(kind, xsrc, ydst, T, first, last):
            for k in range(8):
                dma("sp", pre[:, k, :T], xsrc[k * 128:(k + 1) * 128, :], [], [t_pre[k]])
            P.tag = "ln_in"
            layer_norm(T, "ln_in_g", "ln_in_b")
            for l in range(L):
                block_layer(kind, l, T, first, last)
            if DBG == "dump_o":
                for k in range(8):
                    act(hT[:, k, :T], oT[:, k, :T], AF.Copy, [t_oT[k]], [t_hT[k]])
            for k in range(8):
                dma("sp", ydst[k * 128:(k + 1) * 128, :], hT[:, k, :T], [t_hT[k]], [])

        for j in range(NB):
            run_block("p", xpT[:, j * TB:(j + 1) * TB], ypT[:, j * TB:(j + 1) * TB], TB, j == 0, j == NB - 1)
        run_block("s", xsT, ysT, 128, True, True)

        dma("sp", o_pconv, ctail[:].rearrange("p l c j -> p (l c j)"), t_ctail, [])
        dma("sp", o_plru, hst[:].rearrange("p l c -> p (l c)"), t_hst, [])
        dma("sp", o_pret, S32[:].rearrange("p l c e -> p (l c e)"), t_S32, [])
        dma("sp", o_sconv, s_ctail[:].rearrange("p l c s j -> p (l c s j)"), t_sct, [])
        dma("sp", o_slru, s_hst[:].rearrange("p l c s -> p (l c s)"), t_shst, [])

        P.emit(es)
        build.stats = P.stats
        build.prog = P
    return nc


def _const_tables():
    gam = np.array([1.0 - 2.0 ** (-5.0 - h) for h in range(4)], np.float64)
    slopes = 2.0 ** (-8.0 * np.arange(1, 9) / 8.0)
    tf = {}
    j = np.arange(128)[:, None]
    i = np.arange(128)[None, :]
    dm = np.zeros((128, 4, 128))
    for h in range(4):
        dm[:, h, :] = np.where(i >= j, gam[h] ** np.maximum(i - j, 0), 0.0) / 8.0
    tf["dmask_p"] = dm.reshape(128, 512)
    qd = np.zeros((128, 2, 128))
    for h in range(4):
        pr, hf = divmod(h, 2)
        qd[hf * 64:(hf + 1) * 64, pr, :] = gam[h] ** (np.arange(128) + 1.0)[None, :]
    tf["qdtab_p"] = qd.reshape(128, 256)
    we = np.zeros((128, 4, 64))
    for h in range(4):
        we[:, h, :] = (gam[h] ** (127.0 - np.arange(128)) / 8.0)[:, None]
    tf["wend_p"] = we.reshape(128, 256)
    sj, lj = np.arange(128) // ST, np.arange(128) % ST
    dm = np.zeros((128, 4, 128))
    same = sj[:, None] == sj[None, :]
    dd = lj[None, :] - lj[:, None]
    for h in range(4):
        dm[:, h, :] = np.where(same & (dd >= 0), gam[h] ** np.maximum(dd, 0), 0.0) / 8.0
    tf["dmask_s"] = dm.reshape(128, 512)
    qd = np.zeros((128, 2, 128))
    for h in range(4):
        pr, hf = divmod(h, 2)
        qd[hf * 64:(hf + 1) * 64, pr, :] = gam[h] ** (lj + 1.0)[None, :]
    tf["qdtab_s"] = qd.reshape(128, 256)
    we = np.zeros((128, NSQ, 4, 64))
    for g in range(NSQ):
        for h in range(4):
            we[:, g, h, :] = np.where(sj == g, gam[h] ** (ST - 1.0 - lj) / 8.0, 0.0)[:, None]
    tf["wend_s"] = we.reshape(128, 1024)
    tb = {}
    avg = np.zeros((128, 128))
    avg[0:64, 0:64] = 1.0 / 64
    avg[64:128, 64:128] = 1.0 / 64
    tb["avg_bd"] = avg
    tb["ones_n"] = np.full((128, 128), 1.0 / 1024)
    op = np.zeros((128, 2, 128))
    op[:, 0, 0:64] = 1.0
    op[:, 1, 64:128] = 1.0
    tb["ones_pad"] = op.reshape(128, 256)
    k = np.arange(128)[:, None]
    q = np.arange(128)[None, :]
    bp = np.zeros((2, 128, 8, 128))
    kc, qc = k // 64, q // 64
    dist_prev = (q + 128 - k).astype(np.float64)
    dist_prev = np.where((qc == 1) & (kc == 0), BIG, dist_prev)
    dist_own = np.abs(q - k).astype(np.float64)
    dist_own = np.where((qc == 0) & (kc == 1), BIG, dist_own)
    for h in range(8):
        bp[0, :, h, :] = -8.0 * slopes[h] * dist_prev
        bp[1, :, h, :] = -8.0 * slopes[h] * dist_own
    tb["bias_p"] = bp.transpose(1, 0, 2, 3).reshape(128, 2048)
    bs = np.zeros((5, 128, 8, 128))
    qs, ql = q // ST, q % ST
    for g in range(NSQ):
        dist = (WINDOW + ql - k).astype(np.float64)
        dist = np.where(qs == g, dist, BIG) + 0.0 * k
        for h in range(8):
            bs[g, :, h, :] = -8.0 * slopes[h] * dist
    ks, kl = k // ST, k % ST
    dist = np.abs(ql - kl).astype(np.float64)
    dist = np.where(qs == ks, dist, BIG)
    for h in range(8):
        bs[4, :, h, :] = -8.0 * slopes[h] * dist
    tb["bias_s"] = bs.transpose(1, 0, 2, 3).reshape(128, 5 * 1024)
    tabf = np.concatenate([tf[n] for n, _ in TAB_F32], axis=1).astype(np.float32)
    tabb = np.concatenate([tb[n] for n, _ in TAB_BF], axis=1).astype(np.float32)
    return tabf, tabb


def _chunked(v):
    v = np.asarray(v)
    C = v.shape[-1] // 128
    r = v.reshape(v.shape[:-1] + (C, 128))
    return np.moveaxis(r, -1, 0)


def prepare(inp, TP, L, NCORES, NPS):
    f = np.float32
    voff, NV = vec_layout(L)
    vecs = np.zeros((128, NV), f)

    def put(name, arr):
        vecs[:, voff[name]:voff[name] + arr.shape[1]] = arr

    put("ln_in_g", _chunked(inp["ln_in_g"]))
    put("ln_in_b", _chunked(inp["ln_in_b"]))
    for l in range(L):
        put(f"ln1_g{l}", _chunked(inp["ln1_g"][l]))
        put(f"ln1_b{l}", _chunked(inp["ln1_b"][l]))
        put(f"ln2_g{l}", _chunked(inp["ln2_g"][l]))
        put(f"ln2_b{l}", _chunked(inp["ln2_b"][l]))
        put(f"conv_w{l}", _chunked(inp["conv_w"][l]).reshape(128, 8))
        put(f"conv_b{l}", _chunked(inp["conv_b"][l]))
        put(f"b_a{l}", _chunked(inp["b_rg_a"][l]))
        put(f"b_x{l}", _chunked(inp["b_rg_x"][l]))
        put(f"lam{l}", _chunked(inp["lru_lambda"][l]))
        put(f"gn_g{l}", _chunked(inp["ret_gn_g"][l]))
        put(f"gn_b{l}", _chunked(inp["ret_gn_b"][l]))
        sk = np.repeat(np.asarray(inp["sinks"][l]).reshape(4, 2), 64, axis=1)
        put(f"sinks{l}", np.ascontiguousarray(sk.T))
    tabf, tabb = _const_tables()
    w_in = np.asarray(inp["w_in"])[:L]
    o = np.cumsum([0, 256, 256, 256, 256, 256, 256, 512, 128, 128])
    ax, ag, bq, bk, bv, bg, cq, ck, cv = [w_in[:, :, o[i]:o[i + 1]] for i in range(9)]
    ck0, ck1 = ck[:, :, 0:64], ck[:, :, 64:128]
    w_fm = np.ascontiguousarray(np.concatenate([ax, ag, bq, bk, bg, cq, ck0, ck0, ck1, ck1], axis=2))
    w_tm = np.ascontiguousarray(np.concatenate([bk, bv, cv], axis=2))
    wrg = np.zeros((128, L, 4, 128), f)
    for l in range(L):
        for wi, nm in enumerate(("w_rg_a", "w_rg_x")):
            w = np.asarray(inp[nm][l])
            for ch in range(2):
                for hf in range(2):
                    wrg[hf * 64:(hf + 1) * 64, l, wi * 2 + ch, hf * 64:(hf + 1) * 64] = w[ch * 2 + hf]
    wrg = wrg.reshape(128, L * 4 * 128)
    shared = dict(w_fm=w_fm, w_tm=w_tm, w_out=np.ascontiguousarray(np.asarray(inp["w_out"])[:L]),
                  w_gate=np.ascontiguousarray(np.asarray(inp["w_gate"])[:L]), w_up=np.ascontiguousarray(np.asarray(inp["w_up"])[:L]),
                  w_down=np.ascontiguousarray(np.asarray(inp["w_down"])[:L]), wrg=wrg, vecs=vecs, tabf=tabf, tabb=tabb)
    maps = []
    xp = np.asarray(inp["x_prompt"])
    xs = np.asarray(inp["x_sample"])
    for c in range(NCORES):
        m = dict(shared)
        m["xpT"] = np.ascontiguousarray(xp[c % NPS, :TP].T)
        sq = slice(c * NSQ, (c + 1) * NSQ)
        m["xsT"] = np.ascontiguousarray(xs[sq].reshape(NSQ * ST, D).T)
        sc = np.asarray(inp["state_conv"])[:L, sq]
        m["sconv"] = np.ascontiguousarray(_chunked(sc).transpose(0, 1, 4, 2, 3)).reshape(128, -1)
        sl = np.asarray(inp["state_lru"])[:L, sq]
        m["slru"] = np.ascontiguousarray(_chunked(sl).transpose(0, 1, 3, 2)).reshape(128, -1)
        sr = np.asarray(inp["state_ret"])[:L, sq]
        bd = np.zeros((128, L, NSQ, 2, 128), f)
        for h in range(4):
            pr, hf = divmod(h, 2)
            bd[hf * 64:(hf + 1) * 64, :, :, pr, hf * 64:(hf + 1) * 64] = sr[:, :, h].transpose(2, 0, 1, 3)
        m["sret"] = bd.reshape(128, -1)
        ckc = np.asarray(inp["cache_k"])[:L, sq]
        kd = np.zeros((128, L, NSQ, 2, 128), f)
        for hk in range(2):
            kt = ckc[:, :, :, hk, :].transpose(3, 0, 1, 2)
            kd[0:64, :, :, hk, :] = kt
            kd[64:128, :, :, hk, :] = kt
        m["sck"] = kd.reshape(128, -1)
        cvc = np.asarray(inp["cache_v"])[:L, sq]
        vd = np.zeros((128, L, NSQ, 2, 2, 128), f)
        for hk in range(2):
            vt = cvc[:, :, :, hk, :].transpose(2, 0, 1, 3)
            vd[:, :, :, hk, 0, 0:64] = vt
            vd[:, :, :, hk, 1, 64:128] = vt
        m["scv"] = vd.reshape(128, -1)
        maps.append(m)
    return maps


def _unchunk(a):
    a = np.moveaxis(a, 0, -1)
    return a.reshape(a.shape[:-2] + (a.shape[-2] * 128,))


def assemble(results, TP, L, NCORES, NPS):
    f = np.float32
    B = NPS
    y_prompt = np.stack([results[b]["ypT"].T for b in range(B)]).astype(f)
    y_sample = np.concatenate([results[c]["ysT"].T.reshape(NSQ, ST, D) for c in range(NCORES)]).astype(f)
    p_conv = np.zeros((L, B, 3, 256), f); p_lru = np.zeros((L, B, 256), f); p_ret = np.zeros((L, B, 4, 64, 64), f)
    p_k = np.zeros((L, B, 128, 2, 64), f); p_v = np.zeros((L, B, 128, 2, 64), f)
    for b in range(B):
        r = results[b]
        pc = r["o_pconv"].reshape(128, L, 2, 3)
        p_conv[:, b] = _unchunk(pc.transpose(0, 1, 3, 2))
        p_lru[:, b] = _unchunk(r["o_plru"].reshape(128, L, 2))
        pr_ = r["o_pret"].reshape(128, L, 2, 128)
        for h in range(4):
            pr, hf = divmod(h, 2)
            p_ret[:, b, h] = pr_[hf * 64:(hf + 1) * 64, :, pr, hf * 64:(hf + 1) * 64].transpose(1, 0, 2)
        pk = r["o_pk"].reshape(128, L, 2, 128)
        p_k[:, b] = pk[0:64].transpose(1, 3, 2, 0)
        p_v[:, b] = r["o_pv"].reshape(128, L, 2, 64).transpose(1, 0, 2, 3)
    NSB = NCORES * NSQ
    s_conv = np.zeros((L, NSB, 3, 256), f); s_lru = np.zeros((L, NSB, 256), f); s_ret = np.zeros((L, NSB, 4, 64, 64), f)
    s_k = np.zeros((L, NSB, ST, 2, 64), f); s_v = np.zeros((L, NSB, ST, 2, 64), f)
    for c in range(NCORES):
        r = results[c]
        sq = slice(c * NSQ, (c + 1) * NSQ)
        sc = r["o_sconv"].reshape(128, L, 2, NSQ, 3)
        s_conv[:, sq] = _unchunk(sc.transpose(0, 1, 3, 4, 2))
        s_lru[:, sq] = _unchunk(r["o_slru"].reshape(128, L, 2, NSQ).transpose(0, 1, 3, 2))
        sr = r["o_sret"].reshape(128, L, NSQ, 2, 128)
        for h in range(4):
            pr, hf = divmod(h, 2)
            s_ret[:, sq, h] = sr[hf * 64:(hf + 1) * 64, :, :, pr, hf * 64:(hf + 1) * 64].transpose(1, 2, 0, 3)
        sk = r["o_sk"].reshape(128, L, 2, NSQ, ST)
        s_k[:, sq] = sk[0:64].transpose(1, 3, 4, 2, 0)
        sv = r["o_sv"].reshape(NSQ, ST, L, 2, 64)
        s_v[:, sq] = sv.transpose(2, 0, 1, 3, 4)
    return (y_prompt, y_sample, p_conv, p_lru, p_ret, p_k, p_v, s_conv, s_lru, s_ret, s_k, s_v)


_NC_CACHE = {}


def run(inp, TP, L, NCORES=8, NPS=2):
    key = (TP, L, NCORES)
    if key not in _NC_CACHE:
        _NC_CACHE[key] = build(TP, L, NCORES)
    nc = _NC_CACHE[key]
    maps = prepare(inp, TP, L, NCORES, NPS)
    res = run_bass_kernel_spmd(nc, maps, core_ids=list(range(NCORES)))
    return assemble(res.results, TP, L, NCORES, NPS)


def kernel(**inputs):
    return run(inputs, 8192, DEPTH, 8, 2)
```

```python
import numpy as np
from contextlib import ExitStack
import concourse.bass as bass
import concourse.mybir as mybir
from concourse.bass_utils import run_bass_kernel_spmd

F32 = mybir.dt.float32
BF16 = mybir.dt.bfloat16
AF = mybir.ActivationFunctionType
ALU = mybir.AluOpType

ENGS = ("pe", "dve", "act", "pool", "sp")
SEM_CH = 30000

D = 1024
DEPTH = 4
HD = 64
DFF = 2816
NFF = 22
ALPHA = (2.0 * DEPTH) ** 0.25
EPS = 1e-5
WINDOW = 128
NSQ = 4
ST = 32
BIG = 1.0e6


class Tok:
    __slots__ = ("name", "w", "r", "excl")

    def __init__(self, name="", excl=False):
        self.name = name
        self.w = None
        self.r = {}
        self.excl = excl


class Prog:
    def __init__(self, nc, n_dma_sems, n_slot_sems):
        self.nc = nc
        self.ins = {e: [] for e in ENGS}
        self.seen = {e: {} for e in ENGS}
        self.n_dma_sems = n_dma_sems
        self.n_slot = n_slot_sems
        self.dma_cnt = [0] * n_dma_sems
        self.dma_rr = n_slot_sems
        self.pool_rr = 0
        self.sp_rr = 0
        self.tag = ""
        self.tags = {e: [] for e in ENGS}
        self.rec = []
        self.reorder = True

    def _need(self, eng, ev, deps):
        if ev is None:
            return
        kind, a, b = ev
        if kind == "e" and a == eng and eng in ("pe", "sp"):
            return
        key = (kind, a)
        if self.seen[eng].get(key, -1) >= b:
            return
        self.seen[eng][key] = b
        deps.append(ev)

    def op(self, eng, fn, reads=(), writes=(), dma=False, dma_sem=None, cost=None):
        if cost is None:
            cost = 500.0 if (dma and eng == "pool") else (60.0 if dma else 300.0)
        self.rec.append((eng, fn, tuple(reads), tuple(writes), dma, dma_sem, self.tag, float(cost)))

    def schedule(self):
        import heapq
        rec = self.rec
        n = len(rec)
        preds = [None] * n
        lastw = {}
        readers = {}
        for i, (eng, fn, reads, writes, dma, dsem, tag, cost) in enumerate(rec):
            p = set()
            for t in reads:
                w = lastw.get(id(t))
                if w is not None:
                    p.add(w)
                if t.excl:
                    for j in readers.get(id(t), ()):
                        if rec[j][0] != eng:
                            p.add(j)
            for t in writes:
                w = lastw.get(id(t))
                if w is not None:
                    p.add(w)
                for j in readers.get(id(t), ()):
                    p.add(j)
            p.discard(i)
            preds[i] = p
            for t in writes:
                lastw[id(t)] = i
                readers[id(t)] = []
            for t in reads:
                if t not in writes:
                    readers.setdefault(id(t), []).append(i)
        succs = [[] for _ in range(n)]
        indeg = [0] * n
        for i in range(n):
            indeg[i] = len(preds[i])
            for j in preds[i]:
                succs[j].append(i)
        SEM_LAT = 50.0
        blev = [0.0] * n
        for i in range(n - 1, -1, -1):
            m = 0.0
            for k in succs[i]:
                if blev[k] > m:
                    m = blev[k]
            c = rec[i][7] + (2000.0 if rec[i][4] else 0.0)
            blev[i] = c + m + (SEM_LAT if succs[i] else 0.0)
        fin = [0.0] * n
        start = [0.0] * n
        rdy = [0.0] * n
        pending = {e: [] for e in ENGS}
        avail = {e: [] for e in ENGS}
        free = {e: 0.0 for e in ENGS}
        for i in range(n):
            if indeg[i] == 0:
                heapq.heappush(pending[rec[i][0]], (0.0, i))
        done = 0
        while done < n:
            best = None
            for e in ENGS:
                pe_, av = pending[e], avail[e]
                while pe_ and pe_[0][0] <= free[e]:
                    j_ = heapq.heappop(pe_)[1]
                    heapq.heappush(av, (-blev[j_], j_))
                if av:
                    ts_ = free[e]
                elif pe_:
                    ts_ = pe_[0][0]
                else:
                    continue
                if best is None or ts_ < best[0]:
                    best = (ts_, e)
            ts_, e = best
            if avail[e]:
                i = heapq.heappop(avail[e])[1]
            else:
                i = heapq.heappop(pending[e])[1]
            eng, fn, reads, writes, dma, dsem, tag, cost = rec[i]
            start[i] = ts_
            free[e] = ts_ + cost
            if dma:
                fin[i] = ts_ + cost + 2000.0
            else:
                fin[i] = ts_ + cost
            done += 1
            for k in succs[i]:
                indeg[k] -= 1
                r = fin[i] + SEM_LAT
                if r > rdy[k]:
                    rdy[k] = r
                if indeg[k] == 0:
                    heapq.heappush(pending[rec[k][0]], (rdy[k], k))
        order = sorted(range(n), key=lambda i: (start[i], i))
        self.sim_span = max(fin) if n else 0.0
        for i in order:
            eng, fn, reads, writes, dma, dsem, tag, cost = rec[i]
            self.tag = tag
            self._op(eng, fn, reads, writes, dma, dsem)

    def _op(self, eng, fn, reads=(), writes=(), dma=False, dma_sem=None):
        idx = len(self.ins[eng])
        deps = []
        for t in reads:
            self._need(eng, t.w, deps)
            if t.excl:
                for ev in list(t.r.values()):
                    if not (ev[0] == "e" and ev[1] == eng):
                        self._need(eng, ev, deps)
        for t in writes:
            self._need(eng, t.w, deps)
            for ev in list(t.r.values()):
                self._need(eng, ev, deps)
        dmainfo = None
        if dma:
            if dma_sem is None:
                if eng == "pool":
                    s = self.n_slot + self.pool_rr
                    self.pool_rr = (self.pool_rr + 1) % 8
                else:
                    s = self.n_slot + 8 + self.sp_rr
                    self.sp_rr = (self.sp_rr + 1) % (self.n_dma_sems - self.n_slot - 8)
            else:
                s = dma_sem
            prev = self.dma_cnt[s]
            if prev > 0:
                self._need(eng, ("d", s, prev * 16), deps)
            self.dma_cnt[s] = prev + 1
            ev = ("d", s, (prev + 1) * 16)
            dmainfo = s
        else:
            ev = ("e", eng, idx)
        self.ins[eng].append([fn, deps, dmainfo])
        self.tags[eng].append(self.tag)
        for t in writes:
            t.w = ev
            t.r = {}
        for t in reads:
            if t in writes:
                continue
            t.r[(ev[0], ev[1])] = ev
        return ev

    def emit(self, es):
        nc = self.nc
        if self.reorder:
            self.schedule()
        else:
            for (eng, fn, reads, writes, dma, dsem, tag, cost) in self.rec:
                self.tag = tag
                self._op(eng, fn, reads, writes, dma, dsem)
        final_events = [("d", s, c * 16) for s, c in enumerate(self.dma_cnt) if c > 0]
        marked = {e: set() for e in ENGS}
        for e in ENGS:
            for fn, deps, dmainfo in self.ins[e]:
                for ev in deps:
                    if ev[0] == "e":
                        marked[ev[1]].add(ev[2])
        cnt = {}
        nsem = {}
        for e in ENGS:
            c = 0
            m = {}
            for i in range(len(self.ins[e])):
                if i in marked[e]:
                    c += 1
                    m[i] = c
            cnt[e] = m
            nsem[e] = (c + SEM_CH - 1) // SEM_CH
        esems = {e: [es.enter_context(nc.semaphore(f"s_{e}{k}")) for k in range(nsem[e])] for e in ENGS}
        dsems = [es.enter_context(nc.semaphore(f"s_dma{k}")) for k in range(self.n_dma_sems)]
        self.stats = {e: (len(self.ins[e]), len(marked[e])) for e in ENGS}

        def do_wait(engobj, ev):
            if ev[0] == "e":
                c = cnt[ev[1]][ev[2]]
                k = (c - 1) // SEM_CH
                engobj.wait_ge(esems[ev[1]][k], c - k * SEM_CH)
            else:
                engobj.wait_ge(dsems[ev[1]], ev[2])

        def section(e, engobj):
            for i, (fn, deps, dmainfo) in enumerate(self.ins[e]):
                for ev in deps:
                    do_wait(engobj, ev)
                h = fn(engobj)
                if dmainfo is not None:
                    h.then_inc(dsems[dmainfo], 16)
                elif i in marked[e]:
                    c = cnt[e][i]
                    k = (c - 1) // SEM_CH
                    h.then_inc(esems[e][k], 1)
            if e == "sp":
                for ev in final_events:
                    do_wait(engobj, ev)

        block = es.enter_context(nc.Block())

        @block.tensor
        def _(eng):
            section("pe", eng)

        @block.vector
        def _(eng):
            section("dve", eng)

        @block.scalar
        def _(eng):
            section("act", eng)

        @block.gpsimd
        def _(eng):
            section("pool", eng)

        @block.sync
        def _(eng):
            section("sp", eng)


def vec_layout(L):
    off = {}
    n = 0

    def add(name, k):
        nonlocal n
        off[name] = n
        n += k

    add("ln_in_g", 8)
    add("ln_in_b", 8)
    for l in range(L):
        for nm, k in (("ln1_g", 8), ("ln1_b", 8), ("ln2_g", 8), ("ln2_b", 8), ("conv_w", 8), ("conv_b", 2),
                      ("b_a", 2), ("b_x", 2), ("lam", 2), ("gn_g", 2), ("gn_b", 2), ("sinks", 4)):
            add(f"{nm}{l}", k)
    return off, n


TAB_F32 = [("dmask_p", 512), ("qdtab_p", 256), ("wend_p", 256),
           ("dmask_s", 512), ("qdtab_s", 256), ("wend_s", 1024)]
TAB_BF = [("avg_bd", 128), ("ones_n", 128), ("ones_pad", 256),
          ("bias_p", 2 * 1024), ("bias_s", 5 * 1024)]


def tab_offsets(tabs):
    off = {}
    n = 0
    for nm, k in tabs:
        off[nm] = n
        n += k
    return off, n


TB = 512
STOP = 99
REORDER = True
DBG = ""


def build(TP, L, NCORES=8):
    NB = TP // TB
    NTB_ = TB // 128
    nc = bass.Bass("TRN2", target_bir_lowering=False)

    def din(name, shape):
        return nc.dram_tensor(name, list(shape), F32, kind="ExternalInput").ap()

    def dout(name, shape):
        return nc.dram_tensor(name, list(shape), F32, kind="ExternalOutput").ap()

    voff, NV = vec_layout(L)
    tfo, NTF = tab_offsets(TAB_F32)
    tbo, NTB = tab_offsets(TAB_BF)

    xpT = din("xpT", [D, TP])
    xsT = din("xsT", [D, 128])
    w_fm = din("w_fm", [L, D, 2048])
    w_tm = din("w_tm", [L, D, 640])
    w_out = din("w_out", [L, D, D])
    w_gate = din("w_gate", [L, D, DFF])
    w_up = din("w_up", [L, D, DFF])
    w_down = din("w_down", [L, DFF, D])
    wrg = din("wrg", [128, L * 4 * 128])
    vecs_d = din("vecs", [128, NV])
    tabf_d = din("tabf", [128, NTF])
    tabb_d = din("tabb", [128, NTB])
    sconv_d = din("sconv", [128, L * 2 * NSQ * 3])
    slru_d = din("slru", [128, L * 2 * NSQ])
    sret_d = din("sret", [128, L * NSQ * 2 * 128])
    sck_d = din("sck", [128, L * NSQ * 2 * 128])
    scv_d = din("scv", [128, L * NSQ * 4 * 128])

    ypT = dout("ypT", [D, TP])
    ysT = dout("ysT", [D, 128])
    o_pconv = dout("o_pconv", [128, L * 2 * 3])
    o_plru = dout("o_plru", [128, L * 2])
    o_pret = dout("o_pret", [128, L * 2 * 128])
    o_pk = dout("o_pk", [128, L * 2 * 128])
    o_pv = dout("o_pv", [128, L * 128])
    o_sconv = dout("o_sconv", [128, L * 2 * NSQ * 3])
    o_slru = dout("o_slru", [128, L * 2 * NSQ])
    o_sret = dout("o_sret", [128, L * NSQ * 2 * 128])
    o_sk = dout("o_sk", [128, L * 2 * 128])
    o_sv = dout("o_sv", [128, L * 128])

    NSLOT = 20
    es = ExitStack()
    with es:
        P = Prog(nc, n_dma_sems=NSLOT + 24, n_slot_sems=NSLOT)
        P.reorder = REORDER

        def sb(name, shape, dt=F32):
            return es.enter_context(nc.sbuf_tensor("sb_" + name, list(shape), dt))

        def toks(name, n):
            return [Tok(f"{name}{i}") for i in range(n)]

        vecs = sb("vecs", [128, NV]); t_vecs = Tok("vecs")
        cvec = sb("cvec", [128, L * 4]); t_cvec = Tok("cvec")
        esink = sb("esink", [128, L * 4]); t_esink = Tok("esink")
        tabf = sb("tabf", [128, NTF]); t_tabf = Tok("tabf")
        tabb = sb("tabb", [128, NTB], BF16); t_tabb = Tok("tabb")
        wrgb = sb("wrgb", [128, L * 4 * 128], BF16); t_wrgb = Tok("wrgb")
        hT = sb("hT", [128, 8, TB]); t_hT = toks("hT", 8)
        hb = sb("hb", [128, 8, TB], BF16); t_hb = toks("hb", 8)
        pre = sb("pre", [128, 8, TB]); t_pre = toks("pre", 8)
        hid = sb("hid", [128, 24, TB], BF16); t_hid = toks("hid", 24)
        preb = hid[:, 0:8, :]; t_preb = t_hid[0:8]
        sqb = hid[:, 8:16, :]; t_sqb = t_hid[8:16]
        qr = hid[:, 0:2, :]; t_qr = t_hid[0:2]
        kr = hid[:, 2:4, :]; t_kr = t_hid[2:4]
        qdec = hid[:, 4:6, :]; t_qdec = t_hid[4:6]
        qa = hid[:, 6:10, :]; t_qa = t_hid[6:10]
        ka = hid[:, 10:12, :]; t_ka = t_hid[10:12]
        ubf = hid[:, 12:14, :]; t_ub = t_hid[12:14]
        oT = hid[:, 16:24, :]; t_oT = t_hid[16:24]
        mean_sb = sb("mean_sb", [128, TB]); t_mean = Tok("mean")
        rstd_sb = sb("rstd_sb", [128, TB]); t_rstd = Tok("rstd")
        axbuf = sb("axbuf", [128, 2, TB + 3]); t_ax = toks("ax", 2)
        u32 = sb("u32", [128, 2, TB]); t_u = toks("u", 2)
        rr = pre[:, 0:2, :]; t_rr = t_pre[0:2]
        gi = pre[:, 2:4, :]; t_gi = t_pre[2:4]
        aa = pre[:, 4:6, :]; t_aa = t_pre[4:6]
        bt = pre[:, 6:8, :]; t_bt = t_pre[6:8]
        hh = sb("hh", [128, 2, TB]); t_hh = toks("hh", 2)
        gag = sb("gag", [128, 2, TB]); t_gag = toks("gag", 2)
        sbg = sb("sbg", [128, 2, TB]); t_sbg = toks("sbg", 2)
        kdec = sb("kdec", [128, 4, 256], BF16); t_kdec = toks("kdec", 4)
        vr = sb("vr", [128, NTB_, 256], BF16); t_vr = toks("vr", NTB_)
        vrp = sb("vrp", [128, NTB_, 4, 128], BF16); t_vrp = toks("vrp", NTB_)
        vap = sb("vap", [128, NTB_, 2, 2, 128], BF16); t_vap = toks("vap", NTB_)
        pexp = sb("pexp", [128, 2, 8, 128], BF16); t_pexp = toks("pexp", 2)
        xsc = sb("xsc", [128, 2, 8, 128]); t_xsc = toks("xsc", 2)
        pret = sb("pret", [128, 2, 4, 128], BF16); t_pret = toks("pret", 2)
        ro = sb("ro", [128, 2, TB]); t_ro = toks("ro", 2)
        rob = sb("rob", [128, 2, TB], BF16); t_rob = toks("rob", 2)
        rosq = sb("rosq", [128, 2, TB], BF16); t_rosq = toks("rosq", 2)
        tmpa = mean_sb; t_tmpa = t_mean
        tmpb = rstd_sb; t_tmpb = t_rstd
        rden = sb("rden", [128, 4, 128]); t_rden = Tok("rden")
        sgt = pre[:, 0:4, :]; t_sgt = t_pre[0:4]
        wpool = sb("wpool", [128, NSLOT, 640], BF16); t_w = toks("w", NSLOT)
        ctail = sb("ctail", [128, L, 2, 3]); t_ctail = toks("ctail", L)
        hst = sb("hst", [128, L, 2]); t_hst = toks("hst", L)
        S32 = sb("S32", [128, L, 2, 128]); t_S32 = toks("S32", L)
        Sbf = sb("Sbf", [128, L, 2, 128], BF16); t_Sbf = toks("Sbf", L)
        khalo = sb("khalo", [128, L, 2, 128], BF16); t_khalo = toks("khalo", L)
        vhalo = sb("vhalo", [128, L, 2, 2, 128], BF16); t_vhalo = toks("vhalo", L)
        kouts = sb("kouts", [128, 1, 2, 128]); t_kouts = toks("kouts", 1)
        vouts = sb("vouts", [128, 1, 128]); t_vouts = toks("vouts", 1)
        s_ctail = sb("s_ctail", [128, L, 2, NSQ, 3]); t_sct = toks("sct", L)
        s_hst = sb("s_hst", [128, L, 2, NSQ]); t_shst = toks("shst", L)
        s_S32 = sb("s_S32", [128, 1, NSQ, 2, 128]); t_sS32 = toks("sS32", 1)
        s_Sbf = sb("s_Sbf", [128, 1, NSQ, 2, 128], BF16); t_sSbf = toks("sSbf", 1)
        s_ck = sb("s_ck", [128, 1, NSQ, 2, 128], BF16); t_sck = toks("sck", 1)
        s_cv = sb("s_cv", [128, 1, NSQ, 2, 2, 128], BF16); t_scv = toks("scv", 1)
        s_kouts = kouts; t_skouts = t_kouts
        s_vouts = vouts; t_svouts = t_vouts

        psum = es.enter_context(nc.psum_tensor("psum", [128, 8, 512], F32))
        t_ps = [Tok(f"ps{i}", excl=True) for i in range(8)]
        ps_rr = [0]

        held = set()

        def bank(hold=False):
            while ps_rr[0] in held:
                ps_rr[0] = (ps_rr[0] + 1) % 8
            b = ps_rr[0]
            ps_rr[0] = (b + 1) % 8
            if hold:
                held.add(b)
            return b

        def nfree(ap):
            n_ = 1
            for d_ in ap.shape[1:]:
                n_ *= int(d_)
            return n_

        def act(out, in_, func, reads, writes, **kw):
            P.op("act", lambda e: e.activation(out=out, in_=in_, func=func, **kw), reads, writes, cost=300.0 + nfree(out) / 1.0)

        def tt(out, in0, in1, op, reads, writes):
            P.op("dve", lambda e: e.tensor_tensor(out=out, in0=in0, in1=in1, op=op), reads, writes, cost=150.0 + nfree(out) / 0.9)

        def ts(out, in0, s1, s2, op0, op1, reads, writes):
            if op1 is None:
                P.op("dve", lambda e: e.tensor_scalar(out=out, in0=in0, scalar1=s1, scalar2=None, op0=op0), reads, writes, cost=150.0 + nfree(out) / 0.9)
            else:
                P.op("dve", lambda e: e.tensor_scalar(out=out, in0=in0, scalar1=s1, scalar2=s2, op0=op0, op1=op1), reads, writes, cost=150.0 + nfree(out) / 0.9)

        def stt(out, in0, scalar, in1, op0, op1, reads, writes):
            P.op("dve", lambda e: e.scalar_tensor_tensor(out=out, in0=in0, scalar=scalar, in1=in1, op0=op0, op1=op1), reads, writes, cost=150.0 + nfree(out) / 0.9)

        def dcopy(out, in_, reads, writes):
            P.op("dve", lambda e: e.tensor_copy(out=out, in_=in_), reads, writes, cost=150.0 + nfree(out) / 0.9)

        def mm(out, lhsT, rhs, start, stop, reads, writes, skip=False):
            P.op("pe", lambda e: e.matmul(out, lhsT=lhsT, rhs=rhs, start=start, stop=stop, skip_group_check=skip), reads, writes,
                 cost=25.0 + max(107.0, nfree(rhs) / 2.4) + (110.0 if nfree(rhs) <= 256 else 0.0))

        def dma(eng, out, in_, reads, writes, sem=None):
            return P.op(eng, lambda e: e.dma_start(out=out, in_=in_), reads, writes, dma=True, dma_sem=sem)

        slot_rr = [0]

        def wload(src_ap, ncols):
            s = slot_rr[0]
            slot_rr[0] = (s + 1) % NSLOT
            dma("pool", wpool[:, s, 0:ncols], src_ap, [], [t_w[s]], sem=s)
            return s

        def vcol(name, j=0, n=1):
            o = voff[name] + j
            return vecs[:, o:o + n]

        dma("sp", vecs[:], vecs_d, [], [t_vecs])
        dma("sp", tabf[:], tabf_d, [], [t_tabf])
        dma("pool", tabb[:], tabb_d, [], [t_tabb])
        dma("pool", wrgb[:], wrg, [], [t_wrgb])
        dma("sp", s_ctail[:].rearrange("p l c s j -> p (l c s j)"), sconv_d, [], t_sct)
        dma("sp", s_hst[:].rearrange("p l c s -> p (l c s)"), slru_d, [], t_shst)
        P.op("dve", lambda e: e.memset(vrp[:], 0.0), [], t_vrp)
        P.op("dve", lambda e: e.memset(vap[:], 0.0), [], t_vap)
        P.op("dve", lambda e: e.memset(ctail[:], 0.0), [], t_ctail)
        P.op("dve", lambda e: e.memset(hst[:], 0.0), [], t_hst)
        P.op("dve", lambda e: e.memset(S32[:], 0.0), [], t_S32)
        P.op("dve", lambda e: e.memset(Sbf[:], 0.0), [], t_Sbf)
        for l in range(L):
            lam = vcol(f"lam{l}", 0, 2)
            c1 = cvec[:, l * 4:l * 4 + 2]
            c2 = cvec[:, l * 4 + 2:l * 4 + 4]
            act(c1, lam, AF.Exp, [t_vecs], [t_cvec], scale=-1.0)
            act(c1, c1, AF.Ln, [t_cvec], [t_cvec], bias=1.0, scale=1.0)
            ts(c2, c1, -16.0, None, ALU.mult, None, [t_cvec], [t_cvec])
            ts(c1, c1, -8.0, None, ALU.mult, None, [t_cvec], [t_cvec])
            act(esink[:, l * 4:l * 4 + 4], vcol(f"sinks{l}", 0, 4), AF.Exp, [t_vecs], [t_esink])

        def layer_norm(T, gname, bname):
            ones_n = tabb[:, tbo["ones_n"]:tbo["ones_n"] + 128]
            for k in range(8):
                dcopy(preb[:, k, :T], pre[:, k, :T], [t_pre[k]], [t_preb[k]])
                act(sqb[:, k, :T], pre[:, k, :T], AF.Square, [t_pre[k]], [t_sqb[k]])
            bm = bank()
            for k in range(8):
                mm(psum[:, bm, :T], ones_n, preb[:, k, :T], k == 0, k == 7, [t_tabb, t_preb[k]], [t_ps[bm]])
            bq = bank()
            for k in range(8):
                mm(psum[:, bq, :T], ones_n, sqb[:, k, :T], k == 0, k == 7, [t_tabb, t_sqb[k]], [t_ps[bq]])
            act(mean_sb[:, :T], psum[:, bm, :T], AF.Copy, [t_ps[bm]], [t_mean])
            act(rstd_sb[:, :T], psum[:, bm, :T], AF.Square, [t_ps[bm]], [t_rstd])
            tt(rstd_sb[:, :T], psum[:, bq, :T], rstd_sb[:, :T], ALU.subtract, [t_ps[bq], t_rstd], [t_rstd])
            act(rstd_sb[:, :T], rstd_sb[:, :T], AF.Sqrt, [t_rstd], [t_rstd], bias=EPS, scale=1.0)
            P.op("dve", lambda e: e.reciprocal(out=rstd_sb[:, :T], in_=rstd_sb[:, :T]), [t_rstd], [t_rstd])
            for k in range(8):
                tt(pre[:, k, :T], pre[:, k, :T], mean_sb[:, :T], ALU.subtract, [t_pre[k], t_mean], [t_pre[k]])
                tt(pre[:, k, :T], pre[:, k, :T], rstd_sb[:, :T], ALU.mult, [t_pre[k], t_rstd], [t_pre[k]])
                act(hb[:, k, :T], pre[:, k, :T], AF.Identity, [t_pre[k], t_vecs], [t_hb[k]],
                    scale=vcol(gname, k), bias=vcol(bname, k))
                act(hT[:, k, :T], pre[:, k, :T], AF.Identity, [t_pre[k], t_vecs], [t_hT[k]],
                    scale=vcol(gname, k), bias=vcol(bname, k))

        def block_layer(kind, l, T, first, last):
            NT = T // 128
            samp = kind == "s"
            if samp:
                n1 = NSQ * 2 * 128
                dma("sp", s_S32[:].rearrange("p l s c e -> p (l s c e)"), sret_d[:, l * n1:(l + 1) * n1], [], t_sS32)
                dma("pool", s_Sbf[:].rearrange("p l s c e -> p (l s c e)"), sret_d[:, l * n1:(l + 1) * n1], [], t_sSbf)
                dma("pool", s_ck[:].rearrange("p l s c e -> p (l s c e)"), sck_d[:, l * n1:(l + 1) * n1], [], t_sck)
                dma("pool", s_cv[:].rearrange("p l s c h e -> p (l s c h e)"), scv_d[:, l * 2 * n1:(l + 1) * 2 * n1], [], t_scv)
            if STOP < 2:
                return
            P.tag = "inproj_fm"
            for e in range(16):
                if e % 4 == 0:
                    gb = [bank(hold=True) for _ in range(4)]
                    for k in range(8):
                        sl = wload(w_fm[l, k * 128:(k + 1) * 128, e * 128:(e + 4) * 128], 512)
                        for j in range(4):
                            mm(psum[:, gb[j], :T], wpool[:, sl, j * 128:(j + 1) * 128], hb[:, k, :T], k == 0, k == 7,
                               [t_w[sl], t_hb[k]], [t_ps[gb[j]]])
                    for j in range(4):
                        held.discard(gb[j])
                b = gb[e % 4]
                src = psum[:, b, :T]
                if e < 2:
                    if samp:
                        dst = axbuf[:, e, 0:NSQ * 35].rearrange("p (s j) -> p s j", j=35)[:, :, 3:35]
                        srcv = src.rearrange("p (s j) -> p s j", j=ST)
                    else:
                        dst = axbuf[:, e, 3:3 + T]
                        srcv = src
                    act(dst, srcv, AF.Copy, [t_ps[b]], [t_ax[e]])
                elif e < 4:
                    act(gag[:, e - 2, :T], src, AF.Gelu_apprx_tanh, [t_ps[b]], [t_gag[e - 2]])
                elif e < 6:
                    pr = e - 4
                    if not DBG.endswith("b"):
                        act(qr[:, pr, :T], src, AF.Copy, [t_ps[b]], [t_qr[pr]])
                    tname = "qdtab_s" if samp else "qdtab_p"
                    tab = tabf[:, tfo[tname] + pr * 128: tfo[tname] + (pr + 1) * 128]
                    for t_ in range(NT):
                        if DBG.endswith("a"):
                            continue
                        tt(qdec[:, pr, t_ * 128:(t_ + 1) * 128], psum[:, b, t_ * 128:(t_ + 1) * 128], tab, ALU.mult,
                           [t_ps[b], t_tabf] + ([t_qr[pr]] if DBG.endswith("c") else []), [t_qdec[pr]])
                elif e < 8:
                    act(kr[:, e - 6, :T], src, AF.Copy, [t_ps[b]], [t_kr[e - 6]])
                elif e < 10:
                    act(sbg[:, e - 8, :T], src, AF.Silu, [t_ps[b]], [t_sbg[e - 8]])
                elif e < 14:
                    if e % 2 == 0:
                        act(qa[:, e - 10, :T], src, AF.Copy, [t_ps[b]], [t_qa[e - 10]])
                    else:
                        dcopy(qa[:, e - 10, :T], src, [t_ps[b]], [t_qa[e - 10]])
                else:
                    hk = e - 14
                    dcopy(ka[:, hk, :T], src, [t_ps[b]], [t_ka[hk]])
                    if samp:
                        act(s_kouts[:, 0, hk, :], src, AF.Copy, [t_ps[b]], [t_skouts[0]])
                        if hk == 1:
                            dma("sp", o_sk[:, l * 256:(l + 1) * 256], kouts[:].rearrange("p l c e -> p (l c e)"), t_kouts, [])
                    elif last:
                        act(kouts[:, 0, hk, :], psum[:, b, T - 128:T], AF.Copy, [t_ps[b]], [t_kouts[0]])
                        if hk == 1:
                            dma("sp", o_pk[:, l * 256:(l + 1) * 256], kouts[:].rearrange("p l c e -> p (l c e)"), t_kouts, [])
            if STOP < 3:
                return
            P.tag = "inproj_tm"
            wend_off = tfo["wend_s"] if samp else tfo["wend_p"]
            bAs = [bank(hold=True) for _ in range(NT)]
            bBs = bank(hold=True)
            for k in range(8):
                sl = wload(w_tm[l, k * 128:(k + 1) * 128, :], 640)
                for t in range(NT):
                    mm(psum[:, bAs[t], 0:512], hb[:, k, t * 128:(t + 1) * 128], wpool[:, sl, 0:512], k == 0, k == 7,
                       [t_w[sl], t_hb[k]], [t_ps[bAs[t]]])
                    mm(psum[:, bBs, t * 128:(t + 1) * 128], hb[:, k, t * 128:(t + 1) * 128], wpool[:, sl, 512:640], k == 0 and t == 0, k == 7,
                       [t_w[sl], t_hb[k]], [t_ps[bBs]], skip=True)
            for b_ in bAs:
                held.discard(b_)
            held.discard(bBs)
            for t in range(NT):
                bA = bAs[t]
                if samp:
                    for g in range(NSQ):
                        tt(kdec[:, g, :], psum[:, bA, 0:256], tabf[:, wend_off + g * 256: wend_off + (g + 1) * 256], ALU.mult,
                           [t_ps[bA], t_tabf], [t_kdec[g]])
                else:
                    tt(kdec[:, t, :], psum[:, bA, 0:256], tabf[:, wend_off: wend_off + 256], ALU.mult,
                       [t_ps[bA], t_tabf], [t_kdec[t]])
                act(vr[:, t, :], psum[:, bA, 256:512], AF.Copy, [t_ps[bA]], [t_vr[t]])
                vsrc = psum[:, bA, 256:512].rearrange("p (c h e) -> p c h e", c=2, h=2)
                dcopy(vrp[:, t, 0:4:2, 0:64], vsrc[:, :, 0, :], [t_ps[bA]], [t_vrp[t]])
                dcopy(vrp[:, t, 1:4:2, 64:128], vsrc[:, :, 1, :], [t_ps[bA]], [t_vrp[t]])
                csrc = psum[:, bBs, t * 128:(t + 1) * 128].rearrange("p (h e) -> p h e", h=2)
                dcopy(vap[:, t, :, 0, 0:64], csrc, [t_ps[bBs]], [t_vap[t]])
                dcopy(vap[:, t, :, 1, 64:128], csrc, [t_ps[bBs]], [t_vap[t]])
                if samp:
                    act(s_vouts[:, 0, :], psum[:, bBs, t * 128:(t + 1) * 128], AF.Copy, [t_ps[bBs]], [t_svouts[0]])
                    dma("sp", o_sv[:, l * 128:(l + 1) * 128], vouts[:, 0, :], t_vouts, [])
                elif last and t == NT - 1:
                    act(vouts[:, 0, :], psum[:, bBs, t * 128:(t + 1) * 128], AF.Copy, [t_ps[bBs]], [t_vouts[0]])
                    dma("sp", o_pv[:, l * 128:(l + 1) * 128], vouts[:, 0, :], t_vouts, [])

            if STOP < 4:
                return
            def lru_chunk(ch):
                P.tag = "lru"
                if samp:
                    axv = axbuf[:, ch, 0:NSQ * 35].rearrange("p (s j) -> p s j", j=35)
                    dcopy(axv[:, :, 0:3], s_ctail[:, l, ch, :, :], [t_sct[l]], [t_ax[ch]])
                    sh = lambda j: axv[:, :, j:j + ST]
                    uv = u32[:, ch, :T].rearrange("p (s j) -> p s j", j=ST)
                else:
                    dcopy(axbuf[:, ch, 0:3], ctail[:, l, ch, :], [t_ctail[l]], [t_ax[ch]])
                    sh = lambda j: axbuf[:, ch, j:j + T]
                    uv = u32[:, ch, :T]
                cw = lambda j: vcol(f"conv_w{l}", j * 2 + ch)
                act(uv, sh(3), AF.Identity, [t_ax[ch], t_vecs], [t_u[ch]], scale=cw(3), bias=vcol(f"conv_b{l}", ch))
                for j in (2, 1, 0):
                    stt(uv, sh(j), cw(j), uv, ALU.mult, ALU.add, [t_ax[ch], t_vecs, t_u[ch]], [t_u[ch]])
                if samp:
                    dcopy(s_ctail[:, l, ch, :, :], axv[:, :, ST:ST + 3], [t_ax[ch]], [t_sct[l]])
                else:
                    dcopy(ctail[:, l, ch, :], axbuf[:, ch, T:T + 3], [t_ax[ch]], [t_ctail[l]])
                act(ubf[:, ch, :T], u32[:, ch, :T], AF.Copy, [t_u[ch]], [t_ub[ch]])
                ba = bank()
                mm(psum[:, ba, :T], wrgb[:, (l * 4 + ch) * 128:(l * 4 + ch + 1) * 128], ubf[:, ch, :T], True, True,
                   [t_wrgb, t_ub[ch]], [t_ps[ba]])
                bx = bank()
                mm(psum[:, bx, :T], wrgb[:, (l * 4 + 2 + ch) * 128:(l * 4 + 3 + ch) * 128], ubf[:, ch, :T], True, True,
                   [t_wrgb, t_ub[ch]], [t_ps[bx]])
                act(rr[:, ch, :T], psum[:, ba, :T], AF.Sigmoid, [t_ps[ba], t_vecs], [t_rr[ch]], bias=vcol(f"b_a{l}", ch), scale=1.0)
                act(gi[:, ch, :T], psum[:, bx, :T], AF.Sigmoid, [t_ps[bx], t_vecs], [t_gi[ch]], bias=vcol(f"b_x{l}", ch), scale=1.0)
                act(aa[:, ch, :T], rr[:, ch, :T], AF.Exp, [t_rr[ch], t_cvec], [t_aa[ch]], scale=cvec[:, l * 4 + ch:l * 4 + ch + 1])
                act(bt[:, ch, :T], rr[:, ch, :T], AF.Exp, [t_rr[ch], t_cvec], [t_bt[ch]], scale=cvec[:, l * 4 + 2 + ch:l * 4 + 3 + ch])
                ts(bt[:, ch, :T], bt[:, ch, :T], -1.0, 1.0, ALU.mult, ALU.add, [t_bt[ch]], [t_bt[ch]])
                act(bt[:, ch, :T], bt[:, ch, :T], AF.Sqrt, [t_bt[ch]], [t_bt[ch]])
                tt(gi[:, ch, :T], gi[:, ch, :T], u32[:, ch, :T], ALU.mult, [t_gi[ch], t_u[ch]], [t_gi[ch]])
                tt(bt[:, ch, :T], bt[:, ch, :T], gi[:, ch, :T], ALU.mult, [t_bt[ch], t_gi[ch]], [t_bt[ch]])
                if samp:
                    for s_ in range(NSQ):
                        c0 = s_ * ST
                        P.op("dve", lambda e, c0=c0, s_=s_, ch=ch: e.tensor_tensor_scan(
                            out=hh[:, ch, c0:c0 + ST], data0=aa[:, ch, c0:c0 + ST], data1=bt[:, ch, c0:c0 + ST],
                            initial=s_hst[:, l, ch, s_:s_ + 1], op0=ALU.mult, op1=ALU.add),
                            [t_aa[ch], t_bt[ch], t_shst[l]], [t_hh[ch]], cost=260.0)
                    dcopy(s_hst[:, l, ch, :], hh[:, ch, :T].rearrange("p (s j) -> p s j", j=ST)[:, :, ST - 1], [t_hh[ch]], [t_shst[l]])
                else:
                    P.op("dve", lambda e, ch=ch: e.tensor_tensor_scan(
                        out=hh[:, ch, :T], data0=aa[:, ch, :T], data1=bt[:, ch, :T],
                        initial=hst[:, l, ch:ch + 1], op0=ALU.mult, op1=ALU.add),
                        [t_aa[ch], t_bt[ch], t_hst[l]], [t_hh[ch]], cost=150.0 + 2.1 * T)
                    dcopy(hst[:, l, ch:ch + 1], hh[:, ch, T - 1:T], [t_hh[ch]], [t_hst[l]])
                tt(oT[:, ch, :T], hh[:, ch, :T], gag[:, ch, :T], ALU.mult, [t_hh[ch], t_gag[ch]], [t_oT[ch]])

            dm_off = tfo["dmask_s"] if samp else tfo["dmask_p"]
            dmask = tabf[:, dm_off:dm_off + 512].rearrange("p (h i) -> p h i", h=4)
            gdec = [(1.0 - 2.0 ** (-5.0 - h)) ** (ST if samp else 128) for h in range(4)]
            def ret_tile(t):
                P.tag = "ret"
                cs = slice(t * 128, (t + 1) * 128)
                bsl = [bank(), bank()]
                for h in range(4):
                    pr, hf = divmod(h, 2)
                    rows = slice(hf * 64, hf * 64 + 64)
                    mm(psum[:, bsl[hf], pr * 128:(pr + 1) * 128], kr[rows, pr, cs], qr[rows, pr, cs], True, True,
                       [t_kr[pr], t_qr[pr]], [t_ps[bsl[hf]]])
                pb = t % 2
                for hf in range(2):
                    tt(pret[:, pb, hf:4:2, :], psum[:, bsl[hf], 0:256].rearrange("p (h i) -> p h i", h=2), dmask[:, hf:4:2, :], ALU.mult,
                       [t_ps[bsl[hf]], t_tabf], [t_pret[pb]])
                bo = bank()
                for pr in range(2):
                    oc = slice(pr * 128, (pr + 1) * 128)
                    for hf in range(2):
                        h = pr * 2 + hf
                        mm(psum[:, bo, oc], vrp[:, t, h, :], pret[:, pb, h, :], hf == 0, False,
                           [t_vrp[t], t_pret[pb]], [t_ps[bo]])
                    if samp:
                        for g in range(NSQ):
                            gc = slice(pr * 128 + g * ST, pr * 128 + (g + 1) * ST)
                            mm(psum[:, bo, gc], s_Sbf[:, 0, g, pr, :], qdec[:, pr, g * ST:(g + 1) * ST], False, g == NSQ - 1,
                               [t_sSbf[0], t_qdec[pr]], [t_ps[bo]])
                    else:
                        mm(psum[:, bo, oc], Sbf[:, l, pr, :], qdec[:, pr, cs], False, True,
                           [t_Sbf[l], t_qdec[pr]], [t_ps[bo]])
                act(ro[:, :, cs], psum[:, bo, 0:256].rearrange("p (c i) -> p c i", c=2), AF.Copy, [t_ps[bo]], t_ro)
                ngrp = NSQ if samp else 1
                for g in range(ngrp):
                    bu = bank()
                    kd = kdec[:, g, :] if samp else kdec[:, t, :]
                    for pr in range(2):
                        mm(psum[:, bu, pr * 128:(pr + 1) * 128], kd[:, pr * 128:(pr + 1) * 128], vr[:, t, pr * 128:(pr + 1) * 128],
                           True, True, [t_kdec[g if samp else t], t_vr[t]], [t_ps[bu]])
                    for h in range(4):
                        if DBG == "r3":
                            continue
                        pr, hf = divmod(h, 2)
                        rows = slice(hf * 64, hf * 64 + 64)
                        cols = slice(hf * 64, hf * 64 + 64)
                        if samp:
                            S_ = s_S32[rows, 0, g, pr, cols]
                            Sb_ = s_Sbf[rows, 0, g, pr, cols]
                            tS, tSb = t_sS32[0], t_sSbf[0]
                        else:
                            S_ = S32[rows, l, pr, cols]
                            Sb_ = Sbf[rows, l, pr, cols]
                            tS, tSb = t_S32[l], t_Sbf[l]
                        stt(S_, S_, float(gdec[h]), psum[rows, bu, pr * 128 + hf * 64: pr * 128 + hf * 64 + 64], ALU.mult, ALU.add,
                            [tS, t_ps[bu]], [tS])
                        act(Sb_, S_, AF.Copy, [tS], [tSb])
                if samp:
                    n1 = NSQ * 2 * 128
                    dma("sp", o_sret[:, l * n1:(l + 1) * n1], s_S32[:].rearrange("p l s c e -> p (l s c e)"), t_sS32, [])
            avg = tabb[:, tbo["avg_bd"]:tbo["avg_bd"] + 128]
            def headnorm(c):
                P.tag = "ret"
                act(rob[:, c, :T], ro[:, c, :T], AF.Copy, [t_ro[c]], [t_rob[c]])
                act(rosq[:, c, :T], ro[:, c, :T], AF.Square, [t_ro[c]], [t_rosq[c]])
                bm = bank()
                mm(psum[:, bm, :T], avg, rob[:, c, :T], True, True, [t_tabb, t_rob[c]], [t_ps[bm]])
                bq = bank()
                mm(psum[:, bq, :T], avg, rosq[:, c, :T], True, True, [t_tabb, t_rosq[c]], [t_ps[bq]])
                act(tmpa[:, :T], psum[:, bm, :T], AF.Square, [t_ps[bm]], [t_tmpa])
                tt(tmpa[:, :T], psum[:, bq, :T], tmpa[:, :T], ALU.subtract, [t_ps[bq], t_tmpa], [t_tmpa])
                act(tmpa[:, :T], tmpa[:, :T], AF.Sqrt, [t_tmpa], [t_tmpa], bias=EPS, scale=1.0)
                P.op("dve", lambda e: e.reciprocal(out=tmpa[:, :T], in_=tmpa[:, :T]), [t_tmpa], [t_tmpa])
                tt(tmpb[:, :T], ro[:, c, :T], psum[:, bm, :T], ALU.subtract, [t_ro[c], t_ps[bm]], [t_tmpb])
                tt(tmpb[:, :T], tmpb[:, :T], tmpa[:, :T], ALU.mult, [t_tmpb, t_tmpa], [t_tmpb])
                act(tmpb[:, :T], tmpb[:, :T], AF.Identity, [t_tmpb, t_vecs], [t_tmpb],
                    scale=vcol(f"gn_g{l}", c), bias=vcol(f"gn_b{l}", c))
                tt(oT[:, 2 + c, :T], tmpb[:, :T], sbg[:, c, :T], ALU.mult, [t_tmpb, t_sbg[c]], [t_oT[2 + c]])

            ones_pad = tabb[:, tbo["ones_pad"]:tbo["ones_pad"] + 256].rearrange("p (h e) -> p h e", h=2)
            def attn_tile(t):
                P.tag = "attn"
                cs = slice(t * 128, (t + 1) * 128)
                ktiles = []
                if samp:
                    for g in range(NSQ):
                        bo_ = tbo["bias_s"] + g * 1024
                        ktiles.append((lambda hk, rows, g=g: s_ck[rows, 0, g, hk, :], lambda hk, hf, g=g: s_cv[:, 0, g, hk, hf, :],
                                       tabb[:, bo_:bo_ + 1024], [t_sck[0]], [t_scv[0]]))
                    bo_ = tbo["bias_s"] + 4 * 1024
                    ktiles.append((lambda hk, rows: ka[rows, hk, 0:128], lambda hk, hf: vap[:, 0, hk, hf, :],
                                   tabb[:, bo_:bo_ + 1024], t_ka, [t_vap[0]]))
                else:
                    bo_ = tbo["bias_p"]
                    if t > 0:
                        ktiles.append((lambda hk, rows, t=t: ka[rows, hk, (t - 1) * 128:t * 128], lambda hk, hf, t=t: vap[:, t - 1, hk, hf, :],
                                       tabb[:, bo_:bo_ + 1024], t_ka, [t_vap[t - 1]]))
                    elif not first:
                        ktiles.append((lambda hk, rows: khalo[rows, l, hk, :], lambda hk, hf: vhalo[:, l, hk, hf, :],
                                       tabb[:, bo_:bo_ + 1024], [t_khalo[l]], [t_vhalo[l]]))
                    ktiles.append((lambda hk, rows, t=t: ka[rows, hk, t * 128:(t + 1) * 128], lambda hk, hf, t=t: vap[:, t, hk, hf, :],
                                   tabb[:, bo_ + 1024:bo_ + 2048], t_ka, [t_vap[t]]))
                bn = bank(hold=True)
                bd = bank(hold=True)
                nk = len(ktiles)
                for ki, (kfn, vfn, btab, ktk, vtk) in enumerate(ktiles):
                    pb = (t * 5 + ki) % 2
                    bb = [bank(), bank()]
                    for hq in range(8):
                        cq, hf = divmod(hq, 2)
                        hk = hq // 4
                        rows = slice(hf * 64, hf * 64 + 64)
                        mm(psum[:, bb[hf], cq * 128:(cq + 1) * 128], kfn(hk, rows), qa[rows, cq, cs], True, True,
                           list(ktk) + [t_qa[cq]], [t_ps[bb[hf]]])
                    btab3 = btab.rearrange("p (h q) -> p h q", h=8)
                    for hf in range(2):
                        tt(xsc[:, pb, hf:8:2, :], psum[:, bb[hf], :].rearrange("p (h q) -> p h q", h=4),
                           btab3[:, hf:8:2, :], ALU.add, [t_ps[bb[hf]], t_tabb], [t_xsc[pb]])
                    act(pexp[:, pb, :, :], xsc[:, pb, :, :], AF.Exp, [t_xsc[pb]], [t_pexp[pb]], scale=0.125)
                    for cq in range(4):
                        oc = slice(cq * 128, (cq + 1) * 128)
                        for hf in range(2):
                            hq = cq * 2 + hf
                            hk = hq // 4
                            st_ = ki == 0 and hf == 0 and cq == 0
                            sp_ = ki == nk - 1 and hf == 1
                            mm(psum[:, bn, oc], vfn(hk, hf), pexp[:, pb, hq, :], st_, sp_,
                               list(vtk) + [t_pexp[pb]], [t_ps[bn]], skip=True)
                            mm(psum[:, bd, oc], ones_pad[:, hf, :], pexp[:, pb, hq, :], st_, sp_,
                               [t_tabb, t_pexp[pb]], [t_ps[bd]], skip=True)
                held.discard(bn)
                held.discard(bd)
                for cq in range(4):
                    ts(rden[:, cq, :], psum[:, bd, cq * 128:(cq + 1) * 128], esink[:, l * 4 + cq:l * 4 + cq + 1], None, ALU.add, None,
                       [t_ps[bd], t_esink], [t_rden])
                P.op("dve", lambda e: e.reciprocal(out=rden[:], in_=rden[:]), [t_rden], [t_rden])
                tt(oT[:, 4:8, cs], psum[:, bn, :].rearrange("p (c q) -> p c q", c=4), rden[:], ALU.mult,
                   [t_ps[bn], t_rden], t_oT[4:8])
            for t in range(max(NT, 2)):
                if t < NT:
                    attn_tile(t)
                if t < 2:
                    lru_chunk(t)
                if t < NT:
                    ret_tile(t)
            headnorm(0)
            headnorm(1)
            if not samp and not last:
                dcopy(khalo[:, l, :, :], ka[:, :, T - 128:T], t_ka, [t_khalo[l]])
                act(vhalo[:, l, :, :, :], vap[:, NT - 1, :, :, :], AF.Copy, [t_vap[NT - 1]], [t_vhalo[l]])

            if STOP < 7:
                return
            P.tag = "outproj_ln1"
            for g2 in range(2):
                gb = [bank(hold=True) for _ in range(4)]
                for k in range(8):
                    sl = wload(w_out[l, k * 128:(k + 1) * 128, g2 * 512:(g2 + 1) * 512], 512)
                    for j in range(4):
                        mm(psum[:, gb[j], :T], wpool[:, sl, j * 128:(j + 1) * 128], oT[:, k, :T], k == 0, k == 7,
                           [t_w[sl], t_oT[k]], [t_ps[gb[j]]])
                for j in range(4):
                    held.discard(gb[j])
                    dc = g2 * 4 + j
                    stt(pre[:, dc, :T], hT[:, dc, :T], float(ALPHA), psum[:, gb[j], :T], ALU.mult, ALU.add, [t_hT[dc], t_ps[gb[j]]], [t_pre[dc]])
            layer_norm(T, f"ln1_g{l}", f"ln1_b{l}")

            if STOP < 8:
                return
            P.tag = "ffn_gu"
            for f0 in range(0, NFF, 4):
                nf = min(4, NFF - f0)
                gbk = [bank(hold=True) for _ in range(nf)]
                for k in range(8):
                    sl = wload(w_gate[l, k * 128:(k + 1) * 128, f0 * 128:(f0 + nf) * 128], nf * 128)
                    for j in range(nf):
                        mm(psum[:, gbk[j], :T], wpool[:, sl, j * 128:(j + 1) * 128], hb[:, k, :T], k == 0, k == 7,
                           [t_w[sl], t_hb[k]], [t_ps[gbk[j]]])
                for j in range(nf):
                    held.discard(gbk[j])
                    act(sgt[:, j, :T], psum[:, gbk[j], :T], AF.Silu, [t_ps[gbk[j]]], [t_sgt[j]])
                ubk = [bank(hold=True) for _ in range(nf)]
                for k in range(8):
                    sl = wload(w_up[l, k * 128:(k + 1) * 128, f0 * 128:(f0 + nf) * 128], nf * 128)
                    for j in range(nf):
                        mm(psum[:, ubk[j], :T], wpool[:, sl, j * 128:(j + 1) * 128], hb[:, k, :T], k == 0, k == 7,
                           [t_w[sl], t_hb[k]], [t_ps[ubk[j]]])
                for j in range(nf):
                    held.discard(ubk[j])
                    tt(hid[:, f0 + j, :T], sgt[:, j, :T], psum[:, ubk[j], :T], ALU.mult, [t_sgt[j], t_ps[ubk[j]]], [t_hid[f0 + j]])
            P.tag = "ffn_down"
            for half in range(2):
                banks = [bank() for _ in range(4)]
                for k in range(NFF):
                    s = wload(w_down[l, k * 128:(k + 1) * 128, half * 512:(half + 1) * 512], 512)
                    for j in range(4):
                        mm(psum[:, banks[j], :T], wpool[:, s, j * 128:(j + 1) * 128], hid[:, k, :T], k == 0, k == NFF - 1,
                           [t_w[s], t_hid[k]], [t_ps[banks[j]]])
                for j in range(4):
                    dc = half * 4 + j
                    stt(pre[:, dc, :T], hT[:, dc, :T], float(ALPHA), psum[:, banks[j], :T], ALU.mult, ALU.add,
                        [t_hT[dc], t_ps[banks[j]]], [t_pre[dc]])
            P.tag = "ln2"
            layer_norm(T, f"ln2_g{l}", f"ln2_b{l}")

        def run_block(kind, xsrc, ydst, T, first, last):
            for k in range(8):
                dma("sp", pre[:, k, :T], xsrc[k * 128:(k + 1) * 128, :], [], [t_pre[k]])
            P.tag = "ln_in"
            layer_norm(T, "ln_in_g", "ln_in_b")
            for l in range(L):
                block_layer(kind, l, T, first, last)
            if DBG == "dump_o":
                for k in range(8):
                    act(hT[:, k, :T], oT[:, k, :T], AF.Copy, [t_oT[k]], [t_hT[k]])
            for k in range(8):
                dma("sp", ydst[k * 128:(k + 1) * 128, :], hT[:, k, :T], [t_hT[k]], [])

        for j in range(NB):
            run_block("p", xpT[:, j * TB:(j + 1) * TB], ypT[:, j * TB:(j + 1) * TB], TB, j == 0, j == NB - 1)
        run_block("s", xsT, ysT, 128, True, True)

        dma("sp", o_pconv, ctail[:].rearrange("p l c j -> p (l c j)"), t_ctail, [])
        dma("sp", o_plru, hst[:].rearrange("p l c -> p (l c)"), t_hst, [])
        dma("sp", o_pret, S32[:].rearrange("p l c e -> p (l c e)"), t_S32, [])
        dma("sp", o_sconv, s_ctail[:].rearrange("p l c s j -> p (l c s j)"), t_sct, [])
        dma("sp", o_slru, s_hst[:].rearrange("p l c s -> p (l c s)"), t_shst, [])

        P.emit(es)
        build.stats = P.stats
        build.prog = P
    return nc


def _const_tables():
    gam = np.array([1.0 - 2.0 ** (-5.0 - h) for h in range(4)], np.float64)
    slopes = 2.0 ** (-8.0 * np.arange(1, 9) / 8.0)
    tf = {}
    j = np.arange(128)[:, None]
    i = np.arange(128)[None, :]
    dm = np.zeros((128, 4, 128))
    for h in range(4):
        dm[:, h, :] = np.where(i >= j, gam[h] ** np.maximum(i - j, 0), 0.0) / 8.0
    tf["dmask_p"] = dm.reshape(128, 512)
    qd = np.zeros((128, 2, 128))
    for h in range(4):
        pr, hf = divmod(h, 2)
        qd[hf * 64:(hf + 1) * 64, pr, :] = gam[h] ** (np.arange(128) + 1.0)[None, :]
    tf["qdtab_p"] = qd.reshape(128, 256)
    we = np.zeros((128, 4, 64))
    for h in range(4):
        we[:, h, :] = (gam[h] ** (127.0 - np.arange(128)) / 8.0)[:, None]
    tf["wend_p"] = we.reshape(128, 256)
    sj, lj = np.arange(128) // ST, np.arange(128) % ST
    dm = np.zeros((128, 4, 128))
    same = sj[:, None] == sj[None, :]
    dd = lj[None, :] - lj[:, None]
    for h in range(4):
        dm[:, h, :] = np.where(same & (dd >= 0), gam[h] ** np.maximum(dd, 0), 0.0) / 8.0
    tf["dmask_s"] = dm.reshape(128, 512)
    qd = np.zeros((128, 2, 128))
    for h in range(4):
        pr, hf = divmod(h, 2)
        qd[hf * 64:(hf + 1) * 64, pr, :] = gam[h] ** (lj + 1.0)[None, :]
    tf["qdtab_s"] = qd.reshape(128, 256)
    we = np.zeros((128, NSQ, 4, 64))
    for g in range(NSQ):
        for h in range(4):
            we[:, g, h, :] = np.where(sj == g, gam[h] ** (ST - 1.0 - lj) / 8.0, 0.0)[:, None]
    tf["wend_s"] = we.reshape(128, 1024)
    tb = {}
    avg = np.zeros((128, 128))
    avg[0:64, 0:64] = 1.0 / 64
    avg[64:128, 64:128] = 1.0 / 64
    tb["avg_bd"] = avg
    tb["ones_n"] = np.full((128, 128), 1.0 / 1024)
    op = np.zeros((128, 2, 128))
    op[:, 0, 0:64] = 1.0
    op[:, 1, 64:128] = 1.0
    tb["ones_pad"] = op.reshape(128, 256)
    k = np.arange(128)[:, None]
    q = np.arange(128)[None, :]
    bp = np.zeros((2, 128, 8, 128))
    kc, qc = k // 64, q // 64
    dist_prev = (q + 128 - k).astype(np.float64)
    dist_prev = np.where((qc == 1) & (kc == 0), BIG, dist_prev)
    dist_own = np.abs(q - k).astype(np.float64)
    dist_own = np.where((qc == 0) & (kc == 1), BIG, dist_own)
    for h in range(8):
        bp[0, :, h, :] = -8.0 * slopes[h] * dist_prev
        bp[1, :, h, :] = -8.0 * slopes[h] * dist_own
    tb["bias_p"] = bp.transpose(1, 0, 2, 3).reshape(128, 2048)
    bs = np.zeros((5, 128, 8, 128))
    qs, ql = q // ST, q % ST
    for g in range(NSQ):
        dist = (WINDOW + ql - k).astype(np.float64)
        dist = np.where(qs == g, dist, BIG) + 0.0 * k
        for h in range(8):
            bs[g, :, h, :] = -8.0 * slopes[h] * dist
    ks, kl = k // ST, k % ST
    dist = np.abs(ql - kl).astype(np.float64)
    dist = np.where(qs == ks, dist, BIG)
    for h in range(8):
        bs[4, :, h, :] = -8.0 * slopes[h] * dist
    tb["bias_s"] = bs.transpose(1, 0, 2, 3).reshape(128, 5 * 1024)
    tabf = np.concatenate([tf[n] for n, _ in TAB_F32], axis=1).astype(np.float32)
    tabb = np.concatenate([tb[n] for n, _ in TAB_BF], axis=1).astype(np.float32)
    return tabf, tabb


def _chunked(v):
    v = np.asarray(v)
    C = v.shape[-1] // 128
    r = v.reshape(v.shape[:-1] + (C, 128))
    return np.moveaxis(r, -1, 0)


def prepare(inp, TP, L, NCORES, NPS):
    f = np.float32
    voff, NV = vec_layout(L)
    vecs = np.zeros((128, NV), f)

    def put(name, arr):
        vecs[:, voff[name]:voff[name] + arr.shape[1]] = arr

    put("ln_in_g", _chunked(inp["ln_in_g"]))
    put("ln_in_b", _chunked(inp["ln_in_b"]))
    for l in range(L):
        put(f"ln1_g{l}", _chunked(inp["ln1_g"][l]))
        put(f"ln1_b{l}", _chunked(inp["ln1_b"][l]))
        put(f"ln2_g{l}", _chunked(inp["ln2_g"][l]))
        put(f"ln2_b{l}", _chunked(inp["ln2_b"][l]))
        put(f"conv_w{l}", _chunked(inp["conv_w"][l]).reshape(128, 8))
        put(f"conv_b{l}", _chunked(inp["conv_b"][l]))
        put(f"b_a{l}", _chunked(inp["b_rg_a"][l]))
        put(f"b_x{l}", _chunked(inp["b_rg_x"][l]))
        put(f"lam{l}", _chunked(inp["lru_lambda"][l]))
        put(f"gn_g{l}", _chunked(inp["ret_gn_g"][l]))
        put(f"gn_b{l}", _chunked(inp["ret_gn_b"][l]))
        sk = np.repeat(np.asarray(inp["sinks"][l]).reshape(4, 2), 64, axis=1)
        put(f"sinks{l}", np.ascontiguousarray(sk.T))
    tabf, tabb = _const_tables()
    w_in = np.asarray(inp["w_in"])[:L]
    o = np.cumsum([0, 256, 256, 256, 256, 256, 256, 512, 128, 128])
    ax, ag, bq, bk, bv, bg, cq, ck, cv = [w_in[:, :, o[i]:o[i + 1]] for i in range(9)]
    ck0, ck1 = ck[:, :, 0:64], ck[:, :, 64:128]
    w_fm = np.ascontiguousarray(np.concatenate([ax, ag, bq, bk, bg, cq, ck0, ck0, ck1, ck1], axis=2))
    w_tm = np.ascontiguousarray(np.concatenate([bk, bv, cv], axis=2))
    wrg = np.zeros((128, L, 4, 128), f)
    for l in range(L):
        for wi, nm in enumerate(("w_rg_a", "w_rg_x")):
            w = np.asarray(inp[nm][l])
            for ch in range(2):
                for hf in range(2):
                    wrg[hf * 64:(hf + 1) * 64, l, wi * 2 + ch, hf * 64:(hf + 1) * 64] = w[ch * 2 + hf]
    wrg = wrg.reshape(128, L * 4 * 128)
    shared = dict(w_fm=w_fm, w_tm=w_tm, w_out=np.ascontiguousarray(np.asarray(inp["w_out"])[:L]),
                  w_gate=np.ascontiguousarray(np.asarray(inp["w_gate"])[:L]), w_up=np.ascontiguousarray(np.asarray(inp["w_up"])[:L]),
                  w_down=np.ascontiguousarray(np.asarray(inp["w_down"])[:L]), wrg=wrg, vecs=vecs, tabf=tabf, tabb=tabb)
    maps = []
    xp = np.asarray(inp["x_prompt"])
    xs = np.asarray(inp["x_sample"])
    for c in range(NCORES):
        m = dict(shared)
        m["xpT"] = np.ascontiguousarray(xp[c % NPS, :TP].T)
        sq = slice(c * NSQ, (c + 1) * NSQ)
        m["xsT"] = np.ascontiguousarray(xs[sq].reshape(NSQ * ST, D).T)
        sc = np.asarray(inp["state_conv"])[:L, sq]
        m["sconv"] = np.ascontiguousarray(_chunked(sc).transpose(0, 1, 4, 2, 3)).reshape(128, -1)
        sl = np.asarray(inp["state_lru"])[:L, sq]
        m["slru"] = np.ascontiguousarray(_chunked(sl).transpose(0, 1, 3, 2)).reshape(128, -1)
        sr = np.asarray(inp["state_ret"])[:L, sq]
        bd = np.zeros((128, L, NSQ, 2, 128), f)
        for h in range(4):
            pr, hf = divmod(h, 2)
            bd[hf * 64:(hf + 1) * 64, :, :, pr, hf * 64:(hf + 1) * 64] = sr[:, :, h].transpose(2, 0, 1, 3)
        m["sret"] = bd.reshape(128, -1)
        ckc = np.asarray(inp["cache_k"])[:L, sq]
        kd = np.zeros((128, L, NSQ, 2, 128), f)
        for hk in range(2):
            kt = ckc[:, :, :, hk, :].transpose(3, 0, 1, 2)
            kd[0:64, :, :, hk, :] = kt
            kd[64:128, :, :, hk, :] = kt
        m["sck"] = kd.reshape(128, -1)
        cvc = np.asarray(inp["cache_v"])[:L, sq]
        vd = np.zeros((128, L, NSQ, 2, 2, 128), f)
        for hk in range(2):
            vt = cvc[:, :, :, hk, :].transpose(2, 0, 1, 3)
            vd[:, :, :, hk, 0, 0:64] = vt
            vd[:, :, :, hk, 1, 64:128] = vt
        m["scv"] = vd.reshape(128, -1)
        maps.append(m)
    return maps


def _unchunk(a):
    a = np.moveaxis(a, 0, -1)
    return a.reshape(a.shape[:-2] + (a.shape[-2] * 128,))


def assemble(results, TP, L, NCORES, NPS):
    f = np.float32
    B = NPS
    y_prompt = np.stack([results[b]["ypT"].T for b in range(B)]).astype(f)
    y_sample = np.concatenate([results[c]["ysT"].T.reshape(NSQ, ST, D) for c in range(NCORES)]).astype(f)
    p_conv = np.zeros((L, B, 3, 256), f); p_lru = np.zeros((L, B, 256), f); p_ret = np.zeros((L, B, 4, 64, 64), f)
    p_k = np.zeros((L, B, 128, 2, 64), f); p_v = np.zeros((L, B, 128, 2, 64), f)
    for b in range(B):
        r = results[b]
        pc = r["o_pconv"].reshape(128, L, 2, 3)
        p_conv[:, b] = _unchunk(pc.transpose(0, 1, 3, 2))
        p_lru[:, b] = _unchunk(r["o_plru"].reshape(128, L, 2))
        pr_ = r["o_pret"].reshape(128, L, 2, 128)
        for h in range(4):
            pr, hf = divmod(h, 2)
            p_ret[:, b, h] = pr_[hf * 64:(hf + 1) * 64, :, pr, hf * 64:(hf + 1) * 64].transpose(1, 0, 2)
        pk = r["o_pk"].reshape(128, L, 2, 128)
        p_k[:, b] = pk[0:64].transpose(1, 3, 2, 0)
        p_v[:, b] = r["o_pv"].reshape(128, L, 2, 64).transpose(1, 0, 2, 3)
    NSB = NCORES * NSQ
    s_conv = np.zeros((L, NSB, 3, 256), f); s_lru = np.zeros((L, NSB, 256), f); s_ret = np.zeros((L, NSB, 4, 64, 64), f)
    s_k = np.zeros((L, NSB, ST, 2, 64), f); s_v = np.zeros((L, NSB, ST, 2, 64), f)
    for c in range(NCORES):
        r = results[c]
        sq = slice(c * NSQ, (c + 1) * NSQ)
        sc = r["o_sconv"].reshape(128, L, 2, NSQ, 3)
        s_conv[:, sq] = _unchunk(sc.transpose(0, 1, 3, 4, 2))
        s_lru[:, sq] = _unchunk(r["o_slru"].reshape(128, L, 2, NSQ).transpose(0, 1, 3, 2))
        sr = r["o_sret"].reshape(128, L, NSQ, 2, 128)
        for h in range(4):
            pr, hf = divmod(h, 2)
            s_ret[:, sq, h] = sr[hf * 64:(hf + 1) * 64, :, :, pr, hf * 64:(hf + 1) * 64].transpose(1, 2, 0, 3)
        sk = r["o_sk"].reshape(128, L, 2, NSQ, ST)
        s_k[:, sq] = sk[0:64].transpose(1, 3, 4, 2, 0)
        sv = r["o_sv"].reshape(NSQ, ST, L, 2, 64)
        s_v[:, sq] = sv.transpose(2, 0, 1, 3, 4)
    return (y_prompt, y_sample, p_conv, p_lru, p_ret, p_k, p_v, s_conv, s_lru, s_ret, s_k, s_v)


_NC_CACHE = {}


def run(inp, TP, L, NCORES=8, NPS=2):
    key = (TP, L, NCORES)
    if key not in _NC_CACHE:
        _NC_CACHE[key] = build(TP, L, NCORES)
    nc = _NC_CACHE[key]
    maps = prepare(inp, TP, L, NCORES, NPS)
    res = run_bass_kernel_spmd(nc, maps, core_ids=list(range(NCORES)))
    return assemble(res.results, TP, L, NCORES, NPS)


def kernel(**inputs):
    return run(inputs, 8192, DEPTH, 8, 2)
```
